# Optimizing a Trainium2 kernel written in Bass

```python
import math
import jax, jax.numpy as jnp
from jax import lax
import numpy as np

D_MODEL = 1024
BATCH = 8
SEQ = 4096
DEPTH = 4

A_WIDTH = D_MODEL // 4
A_GROUPS = 4
A_KERNEL = 31
B_HEADS = 4
B_HEAD_DIM = D_MODEL // 16
B_WIDTH = B_HEADS * 2 * B_HEAD_DIM
C_WIDTH = D_MODEL // 4
C_HEADS = 4
C_HEAD_DIM = C_WIDTH // C_HEADS
CHUNK = 128
MIX_WIDTH = A_WIDTH + B_WIDTH + C_WIDTH
IN_WIDTH = 2 * A_WIDTH + 3 * B_WIDTH + 2 * C_WIDTH
FFN_DIM = ((8 * D_MODEL // 3 + 127) // 128) * 128
FFN_KERNEL = 3
ROPE_THETA = 10000.0
ATTN_BLOCK = 128
LN_EPS = 1e-5
DEEPNORM_ALPHA = (2 * DEPTH) ** 0.25
DEEPNORM_BETA = (8 * DEPTH) ** -0.25

kernel_name = "hybrid_conv_diffattn_gmlp_deepnorm"


def layer_norm(x, g, b):
    xf = x.astype(jnp.float32)
    mu = jnp.mean(xf, axis=-1, keepdims=True)
    var = jnp.mean(jnp.square(xf - mu), axis=-1, keepdims=True)
    return ((xf - mu) * lax.rsqrt(var + LN_EPS) * g + b).astype(x.dtype)


def group_norm_channels(x, g, b, groups):
    shp = x.shape
    xf = x.astype(jnp.float32).reshape(shp[:-1] + (groups, shp[-1] // groups))
    mu = jnp.mean(xf, axis=-1, keepdims=True)
    var = jnp.mean(jnp.square(xf - mu), axis=-1, keepdims=True)
    y = ((xf - mu) * lax.rsqrt(var + LN_EPS)).reshape(shp)
    return (y * g + b).astype(x.dtype)


def rms_norm(x, g):
    xf = x.astype(jnp.float32)
    y = xf * lax.rsqrt(jnp.mean(jnp.square(xf), axis=-1, keepdims=True) + LN_EPS)
    return (y * g).astype(x.dtype)


def causal_dwconv(x, w, b):
    k = w.shape[0]
    y = lax.conv_general_dilated(
        x, w[:, None, :].astype(x.dtype), window_strides=(1,), padding=[(k - 1, 0)],
        dimension_numbers=("NWC", "WIO", "NWC"), feature_group_count=x.shape[-1])
    return y + b


def rope_tables(positions):
    inv_freq = 1.0 / (ROPE_THETA ** (jnp.arange(0, B_HEAD_DIM, 2, dtype=jnp.float32) / B_HEAD_DIM))
    ang = positions.astype(jnp.float32)[..., None] * inv_freq
    ang = jnp.concatenate([ang, ang], axis=-1)
    return jnp.cos(ang), jnp.sin(ang)


def apply_rope(x, cos, sin):
    half = x.shape[-1] // 2
    rot = jnp.concatenate([-x[..., half:], x[..., :half]], axis=-1)
    return (x.astype(jnp.float32) * cos + rot.astype(jnp.float32) * sin).astype(x.dtype)


def conformer_conv(z, conv_w, conv_b, gn_g, gn_b):
    a, g = jnp.split(z, 2, axis=-1)
    y = a * jax.nn.sigmoid(g)
    y = causal_dwconv(y, conv_w, conv_b)
    y = group_norm_channels(y, gn_g, gn_b, A_GROUPS)
    return jax.nn.silu(y)


def diff_attention(q, k, v, lam_p, subln_g, cos, sin, lam_init):
    bsz, seq = q.shape[0], q.shape[1]
    q = q.reshape(bsz, seq, B_HEADS, 2, B_HEAD_DIM)
    k = k.reshape(bsz, seq, B_HEADS, 2, B_HEAD_DIM)
    v = v.reshape(bsz, seq, B_HEADS, 2 * B_HEAD_DIM)
    cb, sb = cos[:, :, None, None, :], sin[:, :, None, None, :]
    q = apply_rope(q, cb, sb).transpose(0, 2, 3, 1, 4)
    k = apply_rope(k, cb, sb).transpose(0, 2, 3, 1, 4)
    v = v.transpose(0, 2, 1, 3)
    lp = lam_p.astype(jnp.float32)
    lam = jnp.exp(jnp.sum(lp[0] * lp[1])) - jnp.exp(jnp.sum(lp[2] * lp[3])) + lam_init
    scale = B_HEAD_DIM ** -0.5
    outs = []
    for i in range(seq // ATTN_BLOCK):
        q0 = i * ATTN_BLOCK
        end = q0 + ATTN_BLOCK
        qb = q[:, :, :, q0:end]
        kb = k[:, :, :, :end]
        vb = v[:, :, :end]
        s = jnp.einsum("bhmqd,bhmkd->bhmqk", qb, kb).astype(jnp.float32) * scale
        mask = jnp.arange(end)[None, :] <= (q0 + jnp.arange(ATTN_BLOCK))[:, None]
        p = jax.nn.softmax(jnp.where(mask, s, -jnp.inf), axis=-1)
        a = p[:, :, 0] - lam * p[:, :, 1]
        outs.append(jnp.einsum("bhqk,bhke->bhqe", a.astype(vb.dtype), vb))
    o = jnp.concatenate(outs, axis=2)
    o = (rms_norm(o, subln_g) * (1.0 - lam_init)).astype(v.dtype)
    return o.transpose(0, 2, 1, 3).reshape(bsz, seq, B_WIDTH)


def chunked_spatial_gating(z, ln_g, ln_b, w_sp, b_sp):
    bsz, seq = z.shape[0], z.shape[1]
    z = jax.nn.gelu(z, approximate=False)
    u, v = jnp.split(z, 2, axis=-1)
    v = layer_norm(v, ln_g, ln_b)
    v = v.reshape(bsz, seq // CHUNK, CHUNK, C_HEADS, C_HEAD_DIM)
    w = jnp.where(jnp.tril(jnp.ones((CHUNK, CHUNK), dtype=bool)), w_sp, 0)
    sv = jnp.einsum("hts,bnshd->bnthd", w, v) + b_sp.T[None, None, :, :, None]
    return u * sv.reshape(bsz, seq, C_WIDTH)


def conv_ffn(h, w_up, b_up, conv_w, conv_b, w_down, b_down):
    up = causal_dwconv(h @ w_up + b_up, conv_w, conv_b)
    g, val = jnp.split(up, 2, axis=-1)
    return (jax.nn.silu(g) * val) @ w_down + b_down


def setup_inputs(seed: int = 0) -> dict:
    key = jax.random.key(seed)
    ks = jax.random.split(key, 32)
    f32 = jnp.float32
    nrm = lambda k, shp, s: jax.random.normal(k, shp, f32) * s
    L, D, F = DEPTH, D_MODEL, FFN_DIM
    offset = jax.random.randint(ks[2], (BATCH, 1), 0, 1024, dtype=jnp.int32)
    return {
        "x": nrm(ks[0], (BATCH, SEQ, D), 1.0),
        "c": nrm(ks[1], (BATCH, D), 1.0),
        "positions": offset + jnp.arange(SEQ, dtype=jnp.int32)[None, :],
        "w_ada": nrm(ks[3], (L, D, 6 * D), 0.1 * D ** -0.5),
        "b_ada": nrm(ks[4], (L, 6 * D), 0.01),
        "w_in": nrm(ks[5], (L, D, IN_WIDTH), D ** -0.5),
        "b_in": nrm(ks[6], (L, IN_WIDTH), 0.02),
        "conv_a_w": nrm(ks[7], (L, A_KERNEL, A_WIDTH), A_KERNEL ** -0.5),
        "conv_a_b": nrm(ks[8], (L, A_WIDTH), 0.02),
        "gn_a_g": 1.0 + nrm(ks[9], (L, A_WIDTH), 0.02),
        "gn_a_b": nrm(ks[10], (L, A_WIDTH), 0.02),
        "lam_p": nrm(ks[11], (L, 4, B_HEAD_DIM), 0.1),
        "subln_g": 1.0 + nrm(ks[12], (L, 2 * B_HEAD_DIM), 0.02),
        "ln_c_g": 1.0 + nrm(ks[13], (L, C_WIDTH), 0.02),
        "ln_c_b": nrm(ks[14], (L, C_WIDTH), 0.02),
        "w_sp": nrm(ks[15], (L, C_HEADS, CHUNK, CHUNK), 0.5 * CHUNK ** -0.5),
        "b_sp": 1.0 + nrm(ks[16], (L, C_HEADS, CHUNK), 0.02),
        "w_out": nrm(ks[17], (L, MIX_WIDTH, D), DEEPNORM_BETA * MIX_WIDTH ** -0.5),
        "b_out": nrm(ks[18], (L, D), 0.02),
        "w_up": nrm(ks[19], (L, D, 2 * F), D ** -0.5),
        "b_up": nrm(ks[20], (L, 2 * F), 0.02),
        "conv_f_w": nrm(ks[21], (L, FFN_KERNEL, 2 * F), FFN_KERNEL ** -0.5),
        "conv_f_b": nrm(ks[22], (L, 2 * F), 0.02),
        "w_down": nrm(ks[23], (L, F, D), DEEPNORM_BETA * F ** -0.5),
        "b_down": nrm(ks[24], (L, D), 0.02),
        "ln_g": 1.0 + nrm(ks[25], (L, 2, D), 0.02),
        "ln_b": nrm(ks[26], (L, 2, D), 0.02),
    }


def reference(x, c, positions, w_ada, b_ada, w_in, b_in, conv_a_w, conv_a_b, gn_a_g, gn_a_b,
              lam_p, subln_g, ln_c_g, ln_c_b, w_sp, b_sp, w_out, b_out, w_up, b_up,
              conv_f_w, conv_f_b, w_down, b_down, ln_g, ln_b):
    cos, sin = rope_tables(positions)
    c_act = jax.nn.silu(c)
    splits = [2 * A_WIDTH, 2 * A_WIDTH + B_WIDTH, 2 * A_WIDTH + 2 * B_WIDTH,
              2 * A_WIDTH + 3 * B_WIDTH]
    for l in range(DEPTH):
        lam_init = 0.8 - 0.6 * math.exp(-0.3 * l)
        mod = (c_act @ w_ada[l] + b_ada[l])[:, None, :]
        sh1, sc1, g1, sh2, sc2, g2 = jnp.split(mod, 6, axis=-1)
        h = x * (1.0 + sc1) + sh1
        z = h @ w_in[l] + b_in[l]
        z_a, z_q, z_k, z_v, z_c = jnp.split(z, splits, axis=-1)
        y_a = conformer_conv(z_a, conv_a_w[l], conv_a_b[l], gn_a_g[l], gn_a_b[l])
        y_b = diff_attention(z_q, z_k, z_v, lam_p[l], subln_g[l], cos, sin, lam_init)
        y_c = chunked_spatial_gating(z_c, ln_c_g[l], ln_c_b[l], w_sp[l], b_sp[l])
        y = jnp.concatenate([y_a, y_b, y_c], axis=-1) @ w_out[l] + b_out[l]
        x = layer_norm(DEEPNORM_ALPHA * x + (1.0 + g1) * y, ln_g[l, 0], ln_b[l, 0])
        h = x * (1.0 + sc2) + sh2
        y = conv_ffn(h, w_up[l], b_up[l], conv_f_w[l], conv_f_b[l], w_down[l], b_down[l])
        x = layer_norm(DEEPNORM_ALPHA * x + (1.0 + g2) * y, ln_g[l, 1], ln_b[l, 1])
    return x
```

```python
import math
from contextlib import ExitStack

import numpy as np
import concourse.bass as bass
import concourse.mybir as mybir
from concourse.bass_utils import run_bass_kernel_spmd

F32 = mybir.dt.float32
BF16 = mybir.dt.bfloat16
I32 = mybir.dt.int32
ALU = mybir.AluOpType
AF = mybir.ActivationFunctionType

D = 1024
S = 4096
L = 4
T = 256
NSUB = T // 128
INW = 2560
FF = 2816
NKC = 8
NFC = 2 * FF // 128
NHC = FF // 128
ALPHA = (2 * L) ** 0.25
EPS = 1e-5

B_IN, CAW, CAB, GNG, GNB, SUBG = 0, 20, 82, 84, 86, 88
BOUT, LNG1, LNB1 = 89, 97, 105
BUP, CFW, CFB = 113, 157, 289
BDN, LNG2, LNB2 = 333, 341, 349
BADA = 357
NP = 405
SC1P, G1P, SC2P, G2P, BG1, BG2, NEGLAM, SUBG2, DTMP = 48, 56, 64, 72, 80, 88, 96, 97, 98
CF2 = 104
ND = 148
C_PERM, C_BLK, C_ONES, C_TRI, C_INVF, C_SGN, C_ID = 0, 128, 256, 384, 512, 513, 514
NCM = 642
NPE_CONV = 8

ENGS = ("pe", "act", "dve", "pool", "sp")
ASET = {AF.Exp: "exp", AF.Ln: "ln", AF.Sigmoid: "sig", AF.Gelu: "gelu", AF.Silu: "silu", AF.Sin: "sin"}
TABLE_SWITCH_US = 1.3
SAME_ENG_DIST = 10 ** 9
N_DMA_SEMS = 12


class Op:
    __slots__ = ("eng", "fn", "r", "w", "dma", "eidx", "deps", "need_inc", "inc_val", "dsem",
                 "dval", "waits", "epoch", "fence", "skip", "cost", "lat", "aset")


class Prog:
    def __init__(self):
        import os
        self.ops = []
        self.epoch = 0
        self.max_ops = int(os.environ["KMAXOPS"]) if "KMAXOPS" in os.environ else None
        self.window = int(os.environ.get("KWINDOW", "1500"))
        self.do_sched = os.environ.get("KSCHED", "1") == "1"

    def op(self, eng, fn, r=(), w=(), cost=0.3):
        if self.max_ops is not None and len(self.ops) >= self.max_ops and fn is not None:
            fn = None
            o = Op()
            o.eng, o.fn, o.r, o.w = eng, None, (), ()
            o.dma = False
            o.need_inc = False
            o.epoch = self.epoch
            o.fence = False
            o.skip = True
            self.ops.append(o)
            return o
        o = Op()
        o.skip = False
        o.cost = cost
        o.lat = 0.0
        o.aset = None
        o.eng, o.fn, o.r, o.w = eng, fn, tuple(r), tuple(w)
        o.dma = False
        o.need_inc = False
        o.epoch = self.epoch
        o.fence = False
        self.ops.append(o)
        return o

    def dma(self, eng, fn, r=(), w=(), nbytes=0):
        o = self.op(eng, fn, r, w, cost=0.15)
        o.dma = True
        o.lat = 2.5 + nbytes / 120e3
        return o

    def schedule(self):
        out = []
        seg = []
        for o in self.ops:
            if o.fence or o.skip:
                if seg:
                    out.extend(self._sched_seg(seg))
                    seg = []
                out.append(o)
            else:
                seg.append(o)
        if seg:
            out.extend(self._sched_seg(seg))
        self.ops = out

    def _sched_seg(self, ops):
        n = len(ops)
        idx = {id(o): i for i, o in enumerate(ops)}
        preds = [set() for _ in range(n)]
        last_w = {}
        readers = {}
        for i, o in enumerate(ops):
            for t in o.r:
                lw = last_w.get(t)
                if lw is not None:
                    preds[i].add(lw)
            for t in o.w:
                lw = last_w.get(t)
                if lw is not None:
                    preds[i].add(lw)
                rl = readers.get(t)
                if rl:
                    preds[i].update(rl)
            preds[i].discard(i)
            for t in o.r:
                readers.setdefault(t, []).append(i)
            for t in o.w:
                last_w[t] = i
                readers[t] = []
        succs = [[] for _ in range(n)]
        indeg = [0] * n
        for i in range(n):
            indeg[i] = len(preds[i])
            for p in preds[i]:
                succs[p].append(i)
        finish = [0.0] * n
        ready = [0.0] * n
        eng_free = {e: 0.0 for e in ENGS}
        avail = {e: [] for e in ENGS}
        for i in range(n):
            if indeg[i] == 0:
                avail[ops[i].eng].append(i)
        order = []
        done = 0
        lo = 0
        sched = [False] * n
        W = self.window
        SYNC = 0.25
        cur_set = None
        while done < n:
            best = None
            lim = lo + W
            for e in ENGS:
                ef = eng_free[e]
                for i in avail[e]:
                    if i >= lim:
                        continue
                    st = ready[i] if ready[i] > ef else ef
                    if e == "act":
                        a = ops[i].aset
                        if a is not None and a != cur_set:
                            st += TABLE_SWITCH_US
                    key = (st, i)
                    if best is None or key < best:
                        best = key
            st, i = best
            o = ops[i]
            if o.eng == "act" and o.aset is not None:
                cur_set = o.aset
            avail[o.eng].remove(i)
            eng_free[o.eng] = st + o.cost
            finish[i] = st + o.cost + o.lat
            sched[i] = True
            order.append(o)
            done += 1
            while lo < n and sched[lo]:
                lo += 1
            for j in succs[i]:
                indeg[j] -= 1
                r = finish[i] + (SYNC if ops[j].eng != o.eng or o.dma else 0.0)
                if r > ready[j]:
                    ready[j] = r
                if indeg[j] == 0:
                    avail[ops[j].eng].append(j)
        self.est_time = getattr(self, "est_time", 0.0) + max(finish) if n else 0.0
        return order

    def fence(self):
        for e in ENGS:
            o = self.op(e, None)
            o.fence = True
            o.skip = False
        self.epoch += 1

    def analyze(self):
        last_w, rd_c, rd_d = {}, {}, {}
        ecount = {e: 0 for e in ENGS}
        dcount = {e: 0 for e in ENGS}
        slot_last = {}
        seen = {e: {p: -1 for p in ENGS} for e in ENGS}
        dseen = {e: {} for e in ENGS}
        cnt = {e: 0 for e in ENGS}
        pending = []
        cur_epoch = 0
        for o in self.ops:
            if o.fence or o.skip:
                o.waits = []
                continue
            if o.epoch != cur_epoch:
                cur_epoch = o.epoch
                last_w, rd_c, rd_d = {}, {}, {}
                seen = {e: {p: -1 for p in ENGS} for e in ENGS}
                ecount = {e: 0 for e in ENGS}
            o.eidx = ecount[o.eng]
            ecount[o.eng] += 1
            deps = set()
            for t in o.r:
                lw = last_w.get(t)
                if lw is not None:
                    deps.add(lw)
            for t in o.w:
                lw = last_w.get(t)
                if lw is not None:
                    deps.add(lw)
                d = rd_c.get(t)
                if d:
                    deps.update(d.values())
                l = rd_d.get(t)
                if l:
                    deps.update(l)
            if o.dma:
                k = dcount[o.eng] % N_DMA_SEMS
                dcount[o.eng] += 1
                key = (o.eng, k)
                prev = slot_last.get(key)
                if prev is not None:
                    deps.add(prev)
                    o.dval = prev.dval + 16
                else:
                    o.dval = 16
                o.dsem = key
                slot_last[key] = o
            deps.discard(o)
            for t in o.r:
                if o.dma:
                    rd_d.setdefault(t, []).append(o)
                else:
                    rd_c.setdefault(t, {})[o.eng] = o
            for t in o.w:
                last_w[t] = o
                rd_c[t] = {}
                rd_d[t] = []
            best = {}
            dm = {}
            for d in deps:
                if d.dma:
                    if d.dval > dseen[o.eng].get(d.dsem, 0):
                        if d.dsem not in dm or d.dval > dm[d.dsem].dval:
                            dm[d.dsem] = d
                else:
                    if d.eng == o.eng and not o.dma:
                        if o.eng == "pe":
                            continue
                        if o.eidx - d.eidx > SAME_ENG_DIST:
                            continue
                    b = best.get(d.eng)
                    if b is None or d.eidx > b.eidx:
                        best[d.eng] = d
            waits = []
            for d in dm.values():
                dseen[o.eng][d.dsem] = d.dval
                waits.append(d)
            for p, d in best.items():
                if d.eidx <= seen[o.eng][p]:
                    continue
                seen[o.eng][p] = d.eidx
                d.need_inc = True
                waits.append(d)
            o.waits = waits
        for o in self.ops:
            if o.fence:
                cnt = {e: 0 for e in ENGS}
                continue
            if o.skip:
                continue
            if (not o.dma) and o.need_inc:
                cnt[o.eng] += 1
                o.inc_val = cnt[o.eng]
        self.slot_last = slot_last

    def emit(self, nc, es):
        if self.do_sched:
            self.schedule()
        self.analyze()
        nep = self.epoch + 1
        csem = {(e, k): es.enter_context(nc.semaphore("c_%s_%d" % (e, k))) for e in ENGS for k in range(nep)}
        fsem = es.enter_context(nc.semaphore("fence"))
        dsem = {key: es.enter_context(nc.semaphore("d_%s_%d" % key)) for key in self.slot_last}
        by_eng = {e: [] for e in ENGS}
        for o in self.ops:
            by_eng[o.eng].append(o)

        def run(ename, eng):
            issued = {}
            nf = 0
            for o in by_eng[ename]:
                if o.fence:
                    for key, val in issued.items():
                        eng.wait_ge(dsem[key], val)
                    nf += 1
                    eng.drain().then_inc(fsem, 1)
                    eng.wait_ge(fsem, len(ENGS) * nf)
                    continue
                if o.skip:
                    continue
                for d in o.waits:
                    if d.dma:
                        eng.wait_ge(dsem[d.dsem], d.dval)
                    else:
                        eng.wait_ge(csem[(d.eng, d.epoch)], d.inc_val)
                ins = o.fn(eng)
                if o.dma:
                    ins.then_inc(dsem[o.dsem], 16)
                    issued[o.dsem] = o.dval
                elif o.need_inc:
                    ins.then_inc(csem[(ename, o.epoch)], 1)
            for key, val in issued.items():
                eng.wait_ge(dsem[key], val)

        with nc.Block() as block:
            @block.tensor
            def _(e):
                run("pe", e)

            @block.scalar
            def _(e):
                run("act", e)

            @block.vector
            def _(e):
                run("dve", e)

            @block.gpsimd
            def _(e):
                run("pool", e)

            @block.sync
            def _(e):
                run("sp", e)


class Arena:
    def __init__(self, ap_f32, nbytes):
        self.ap = ap_f32
        self.nbytes = nbytes
        self.off = 0

    def reset(self, off=0):
        self.off = off

    def take(self, shape, dt):
        esz = 4 if dt in (F32, I32) else 2
        n = 1
        for s in shape[1:]:
            n *= s
        nb = (n * esz + 31) // 32 * 32
        a = self.off
        assert a + nb <= self.nbytes, ("arena overflow", a, nb, self.nbytes)
        self.off += nb
        v = self.ap[:, a // 4:(a + nb) // 4]
        if dt != F32:
            v = v.bitcast(dt)
        v = v[:, 0:n]
        if len(shape) == 3:
            v = v.rearrange("p (a b) -> p a b", a=shape[1])
        elif len(shape) == 4:
            v = v.rearrange("p (a b c) -> p a b c", a=shape[1], b=shape[2])
        return v


def build(n_layers=L, n_tiles=S // T, stop_phase=None):
    nc = bass.Bass("TRN2", target_bir_lowering=False)
    dt_in = lambda name, shape, dt=F32: nc.dram_tensor(name, shape, dt, kind="ExternalInput").ap()
    xT = dt_in("xT", [D, S])
    yT = nc.dram_tensor("yT", [D, S], F32, kind="ExternalOutput").ap()
    xa = nc.dram_tensor("xa", [D, S], F32).ap()
    xb = nc.dram_tensor("xb", [D, S], F32).ap()
    cst = nc.dram_tensor("cst", [2, 128, S], F32).ap()
    pos = dt_in("pos", [1, S], I32)
    cvec = dt_in("cvec", [128, NKC])
    cm_d = dt_in("cm", [128, NCM])
    pp_d = dt_in("pp", [128, L, NP])
    rows_d = dt_in("rows", [L, 128, 1280])
    lamp_d = dt_in("lamp", [L, 128, 256])
    wsp_d = dt_in("wspT", [L, 128, 4, 128])
    bsp_d = dt_in("bsp", [L, 128, 2, 128])
    w_ada = dt_in("w_ada", [L, D, 6 * D])
    w_in = dt_in("w_in", [L, D, INW])
    w_out = dt_in("w_out", [L, D, D])
    w_up = dt_in("w_up", [L, D, 2 * FF])
    w_dn = dt_in("w_down", [L, FF, D])

    P = Prog()
    es = ExitStack()
    es.enter_context(nc.allow_low_precision("bf16 matmul operands, fp32 PSUM accumulation"))
    sb = lambda name, shape, dt: es.enter_context(nc.sbuf_tensor(name, shape, dt))
    cm = sb("cm_sb", [128, NCM], F32)
    pp = sb("pp_sb", [128, L, NP], F32)
    dv = sb("dv", [128, L, ND], F32)
    ones_bf = sb("ones_bf", [128, 128], BF16)
    blk_bf = sb("blk_bf", [128, 128], BF16)
    tri2_bf = sb("tri2_bf", [128, 2, 128], BF16)
    cact = sb("cact", [128, NKC], BF16)
    ARENA_BYTES = 195 * 1024
    arena_t = sb("arena", [128, ARENA_BYTES // 4], F32)
    A = Arena(arena_t, ARENA_BYTES)
    ps = [es.enter_context(nc.psum_tensor("ps%d" % i, [128, 512], F32)) for i in range(8)]
    pst = lambda b: ("ps", b)
    perm = cm[:, C_PERM:C_PERM + 128]
    tri_f = cm[:, C_TRI:C_TRI + 128]
    invf = cm[:, C_INVF:C_INVF + 1]
    sgn = cm[:, C_SGN:C_SGN + 1]

    rot = {"n": 0, "s": 0, "b": 0, "f": 0}

    def bank():
        rot["n"] += 1
        return 6 + rot["n"] % 2

    def sbank():
        rot["s"] += 1
        return 2 + rot["s"] % 4

    def bbank():
        rot["b"] += 1
        return rot["b"] % 2

    def fbank():
        rot["f"] += 1
        return 2 + rot["f"] % 6

    ew = {"n": 0}

    def ve():
        ew["n"] += 1
        return "dve" if ew["n"] % 2 else "pool"

    def fsz(ap):
        n = 1
        for d in ap.shape[1:]:
            n *= d
        return n

    def mm(out, lhsT, rhs, start, stop, r, w):
        c = 0.02 + fsz(out) / 2400.0
        if rhs.dtype == F32:
            c *= 4
        P.op("pe", lambda e: e.matmul(out, lhsT, rhs, start=start, stop=stop), r, w, cost=c)

    def act(out, in_, func, r, w, bias=None, scale=None, accum=None):
        kw = {}
        if bias is not None:
            kw["bias"] = bias
        if scale is not None:
            kw["scale"] = scale
        if accum is not None:
            kw["accum_out"] = accum
        o = P.op("act", lambda e: e.activation(out=out, in_=in_, func=func, **kw), r, w, cost=(fsz(out) + 335) / 1200.0)
        o.aset = ASET.get(func)

    def vcost(eng, out):
        c = (fsz(out) + 151) / 960.0
        return c * 2.0 if eng == "pool" else c

    def ts(eng, out, in0, s1, s2, op0, op1, r, w):
        if s2 is None:
            P.op(eng, lambda e: e.tensor_scalar(out=out, in0=in0, scalar1=s1, scalar2=None, op0=op0), r, w,
                 cost=vcost(eng, out))
        else:
            P.op(eng, lambda e: e.tensor_scalar(out=out, in0=in0, scalar1=s1, scalar2=s2, op0=op0, op1=op1), r, w,
                 cost=vcost(eng, out))

    def tt(eng, out, in0, in1, op, r, w):
        P.op(eng, lambda e: e.tensor_tensor(out=out, in0=in0, in1=in1, op=op), r, w, cost=vcost(eng, out))

    def stt(eng, out, in0, scalar, in1, op0, op1, r, w):
        eng = "dve"
        P.op(eng, lambda e: e.scalar_tensor_tensor(out=out, in0=in0, scalar=scalar, in1=in1, op0=op0, op1=op1), r, w,
             cost=vcost(eng, out))

    def cp(eng, out, in_, r, w):
        P.op(eng, lambda e: e.tensor_copy(out=out, in_=in_), r, w, cost=vcost(eng, out))

    def dma(q, out, in_, r, w):
        nb = 128 * fsz(out) * (4 if out.dtype in (F32, I32) else 2)
        P.dma(q, lambda e: e.dma_start(out=out, in_=in_), r, w, nbytes=nb)

    dma("sp", cm[:], cm_d, [], ["cm"])
    dma("sp", pp[:], pp_d, [], ["pp"])
    cp("dve", ones_bf[:], cm[:, C_ONES:C_ONES + 128], ["cm"], ["ones_bf"])
    cp("dve", blk_bf[:], cm[:, C_BLK:C_BLK + 128], ["cm"], ["blk_bf"])
    cp("dve", tri2_bf[:, 0, :], tri_f, ["cm"], ["tri2_bf"])
    cp("dve", tri2_bf[:, 1, :], tri_f, ["cm"], ["tri2_bf"])
    A.reset()
    posi = A.take([128, S], I32)
    CB = 1024
    tA = A.take([128, CB], F32)
    tF = A.take([128, CB], F32)
    tN = A.take([128, CB], F32)
    tNi = A.take([128, CB], I32)
    tR = A.take([128, CB], F32)
    tS = A.take([128, CB], F32)
    tC = A.take([128, CB], F32)
    tO1 = A.take([128, CB], F32)
    tO2 = A.take([128, CB], F32)
    cvs = A.take([128, NKC], F32)
    wst = [A.take([128, NKC, 512], BF16) for _ in range(2)]
    dma("sp", posi, pos.broadcast_to([128, S]), [], ["posi"])
    INV2PI = 1.0 / (2.0 * math.pi)
    C1 = 6.28125
    C2 = 2.0 * math.pi - C1
    for cb in range(S // CB):
        sl = slice(cb * CB, (cb + 1) * CB)
        cp("dve", tA, posi[:, sl], ["posi"], ["tA"])
        ts("dve", tA, tA, invf, None, ALU.mult, None, ["tA", "cm"], ["tA"])
        ts("dve", tF, tA, INV2PI, None, ALU.mult, None, ["tA"], ["tF"])
        cp("dve", tNi, tF, ["tF"], ["tNi"])
        cp("dve", tN, tNi, ["tNi"], ["tN"])
        stt("dve", tR, tN, -C1, tA, ALU.mult, ALU.add, ["tN", "tA"], ["tR"])
        stt("dve", tR, tN, -C2, tR, ALU.mult, ALU.add, ["tN", "tR"], ["tR"])
        ts("dve", tS, tR, -math.pi, math.pi, ALU.max, ALU.min, ["tR"], ["tS"])
        act(tO2, tS, AF.Sin, ["tS", "cm"], ["tO2"], scale=sgn)
        ts("dve", tC, tR, math.pi / 2, None, ALU.add, None, ["tR"], ["tC"])
        ts("dve", tF, tC, math.pi, -2.0 * math.pi, ALU.is_gt, ALU.mult, ["tC"], ["tF"])
        tt("dve", tC, tC, tF, ALU.add, ["tC", "tF"], ["tC"])
        ts("dve", tC, tC, -math.pi, math.pi, ALU.max, ALU.min, ["tC"], ["tC"])
        act(tO1, tC, AF.Sin, ["tC"], ["tO1"])
        dma("sp", cst[0, :, sl], tO1, ["tO1"], [("cst", cb)])
        dma("sp", cst[1, :, sl], tO2, ["tO2"], [("cst", cb)])
    dma("sp", cvs, cvec, [], ["cvs"])
    act(cact[:], cvs, AF.Silu, ["cvs"], ["cact"])
    nblk = 0
    for l in range(n_layers):
        pb = l % 4
        for j in range(12):
            wb = wst[nblk % 2]
            wtok = ("wst", nblk % 2)
            nblk += 1
            dma("pool", wb, w_ada[l].rearrange("(kc p) n -> p kc n", p=128)[:, :, j * 512:(j + 1) * 512],
                [], [wtok])
            for fc in range(4):
                col = j * 4 + fc
                for kc in range(NKC):
                    mm(ps[pb][:, col:col + 1], wb[:, kc, fc * 128:(fc + 1) * 128], cact[:, kc:kc + 1],
                       kc == 0, kc == NKC - 1, [wtok, "cact"], [pst(pb)])
        dvl = ("dv", l)
        tt("dve", dv[:, l, 0:48], ps[pb][:, 0:48], pp[:, l, BADA:BADA + 48], ALU.add, ["pp"], [pst(pb), dvl])
        ts("dve", dv[:, l, SC1P:SC1P + 8], dv[:, l, 8:16], 1.0, None, ALU.add, None, [dvl], [(dvl, 1)])
        ts("pool", dv[:, l, G1P:G1P + 8], dv[:, l, 16:24], 1.0, None, ALU.add, None, [dvl], [(dvl, 2)])
        ts("dve", dv[:, l, SC2P:SC2P + 8], dv[:, l, 32:40], 1.0, None, ALU.add, None, [dvl], [(dvl, 3)])
        ts("pool", dv[:, l, G2P:G2P + 8], dv[:, l, 40:48], 1.0, None, ALU.add, None, [dvl], [(dvl, 4)])
        tt("dve", dv[:, l, BG1:BG1 + 8], dv[:, l, G1P:G1P + 8], pp[:, l, BOUT:BOUT + 8], ALU.mult,
           [(dvl, 2), "pp"], [(dvl, 5)])
        tt("dve", dv[:, l, BG2:BG2 + 8], dv[:, l, G2P:G2P + 8], pp[:, l, BDN:BDN + 8], ALU.mult,
           [(dvl, 4), "pp"], [(dvl, 6)])
        lam_init = 0.8 - 0.6 * math.exp(-0.3 * l)
        ts("pool", dv[:, l, SUBG2:SUBG2 + 1], pp[:, l, SUBG:SUBG + 1], 1.0 - lam_init, None, ALU.mult, None,
           ["pp"], [(dvl, 7)])
        tt("dve", dv[:, l, CF2:CF2 + 44], pp[:, l, CFW + 2:CFW + 132:3], pp[:, l, BUP:BUP + 44], ALU.mult,
           ["pp"], [(dvl, 8)])
        tt("dve", dv[:, l, CF2:CF2 + 44], dv[:, l, CF2:CF2 + 44], pp[:, l, CFB:CFB + 44], ALU.add,
           ["pp", (dvl, 8)], [(dvl, 8)])

    for l in range(n_layers):
        lam_init = 0.8 - 0.6 * math.exp(-0.3 * l)
        x_in = xT if l == 0 else xb
        last = (l == n_layers - 1)
        x_mid = yT if stop_phase == (l, 0) else xa
        x_out = yT if ((last and stop_phase is None) or stop_phase == (l, 1)) else xb
        col = lambda c0, n=1: pp[:, l, c0:c0 + n]
        dcol = lambda c0, n=1: dv[:, l, c0:c0 + n]
        P.fence()
        A.reset()
        kT = A.take([128, 4, S], BF16)
        Vc = A.take([128, S // 128, 512], BF16)
        win = A.take([128, NKC, INW], BF16)
        wout = A.take([128, NKC, D], BF16)
        rows = A.take([128, 1280], F32)
        wsp = A.take([128, 4, 128], BF16)
        bsp = A.take([128, 2, 128], F32)
        xt2 = [A.take([128, NKC, T], F32) for _ in range(2)]
        hb = A.take([128, NKC, T], BF16)
        cs_c = A.take([128, T], F32)
        cs_s = A.take([128, T], F32)
        yglu = A.take([128, 2, 30 + T], F32)
        cacc = A.take([128, 2, 2, T], F32)
        ftmp = [A.take([128, T], F32) for _ in range(6)]
        ftbf = [A.take([128, T], BF16) for _ in range(2)]
        off_tmp = A.off
        tmp = [A.take([128, T], F32) for _ in range(6)]
        off_end = A.off
        A.reset(off_tmp)
        OL = [A.take([128, 2 * T], F32) for _ in range(2)]
        A.reset(off_tmp)
        wspf = A.take([128, 4, 128], F32)
        lam_t = A.take([128, 256], F32)
        lam_w = A.take([128, 128], F32)
        assert A.off <= off_tmp + 4 * T * 4
        A.reset(off_end)
        yglu_bf = A.take([128, 2, 30 + T], BF16)
        tbf = [A.take([128, T], BF16) for _ in range(4)]
        mhalf = A.take([128, T], F32)
        qT2 = [A.take([128, 4, T], BF16) for _ in range(2)]
        uT = A.take([128, 2, T], F32)
        vcw = [A.take([128, 256], F32) for _ in range(2)]
        vn = A.take([128, NSUB, 256], BF16)
        st4 = A.take([128, 8], F32)
        PT = [A.take([128, 2, T], BF16) for _ in range(4)]
        ycat2 = [A.take([128, NKC, T], BF16) for _ in range(2)]

        for j in range(5):
            dma("pool", win[:, :, j * 512:(j + 1) * 512],
                w_in[l].rearrange("(kc p) n -> p kc n", p=128)[:, :, j * 512:(j + 1) * 512], [], [("win", j)])
        for kc in range(NKC):
            dma("pool", wout[:, kc, :], w_out[l, kc * 128:(kc + 1) * 128, :], [], [("wout", kc)])
        dma("sp", rows, rows_d[l], [], ["rows"])
        dma("sp", bsp, bsp_d[l], [], ["bsp"])
        dma("sp", wspf, wsp_d[l], [], ["tmp0", "tmp1"])
        for hh in range(4):
            tt("dve", wsp[:, hh, :], wspf[:, hh, :], tri_f, ALU.mult, ["tmp0", "tmp1", "cm"], [("wsp", hh)])
        dma("sp", lam_t, lamp_d[l], [], ["tmp2"])
        tt("dve", lam_w[:, 0:64], lam_t[:, 0:64], lam_t[:, 64:128], ALU.mult, ["tmp2"], ["tmp3"])
        tt("dve", lam_w[:, 64:128], lam_t[:, 128:192], lam_t[:, 192:256], ALU.mult, ["tmp2", "tmp3"], ["tmp3"])
        P.op("dve", lambda e, o=st4[:, 0:1], i=lam_w[:, 0:64]: e.tensor_reduce(
            out=o, in_=i, axis=mybir.AxisListType.X, op=ALU.add), ["tmp3"], ["st4a"])
        P.op("dve", lambda e, o=st4[:, 1:2], i=lam_w[:, 64:128]: e.tensor_reduce(
            out=o, in_=i, axis=mybir.AxisListType.X, op=ALU.add), ["tmp3"], ["st4b"])
        act(st4[:, 2:3], st4[:, 0:1], AF.Exp, ["st4a"], ["st4c"])
        act(st4[:, 3:4], st4[:, 1:2], AF.Exp, ["st4b"], ["st4d"])
        stt("dve", dcol(NEGLAM), st4[:, 3:4], -lam_init, st4[:, 2:3], ALU.add, ALU.subtract,
            ["st4c", "st4d"], [("dv", l, "neglam")])
        P.op("pool", lambda e, a=yglu[:, :, 0:30]: e.memset(a, 0.0), [], [("yglu", 0), ("yglu", 1)])
        P.op("pool", lambda e, a=yglu_bf[:, :, 0:30]: e.memset(a, 0.0), [], [("yglub", 0), ("yglub", 1)])
        P.op("pool", lambda e, a=mhalf: e.memset(a, -0.5), [], ["mhalf"])
        dg = Vc[:, 16:32, :].rearrange("p a b -> p (a b)")[:, 0:62 * 128].rearrange("p (k c) -> p k c", k=62)
        ident = cm[:, C_ID:C_ID + 128]
        for idx in range(62):
            ts("dve", dg[:, idx, :], ident, col(CAW + idx), None, ALU.mult, None, ["cm", "pp"], ["dg"])

        for i in range(n_tiles):
            t0 = i * T
            tsl = slice(t0, t0 + T)
            par = i % 2
            xt = xt2[par]
            qT = qT2[par]
            ycat = ycat2[par]
            XT = lambda c, par=par: ("xt", par, c)
            QT = lambda hd, par=par: ("qT", par, hd)
            YC = lambda c, par=par: ("ycat", par, c)
            dma("sp", xt, x_in.rearrange("(c p) t -> p c t", p=128)[:, :, tsl], [("xd", l, 0, i)],
                [XT(c) for c in range(NKC)])
            dma("sp", cs_c, cst[0, :, tsl], [("cst", t0 // CB)], ["cs_c"])
            dma("sp", cs_s, cst[1, :, tsl], [("cst", t0 // CB)], ["cs_s"])
            for c in range(NKC):
                ts(ve(), hb[:, c, :], xt[:, c, :], dcol(SC1P + c), dcol(c), ALU.mult, ALU.add,
                   [XT(c), ("dv", l), (("dv", l), 1)], [("hb", c)])
            hr = [("hb", c) for c in range(NKC)]

            def proj_fm(fc, pb):
                for kc in range(NKC):
                    mm(ps[pb][:, 0:T], win[:, kc, fc * 128:(fc + 1) * 128], hb[:, kc, :], kc == 0, kc == NKC - 1,
                       [("win", fc // 4), ("hb", kc)], [pst(pb)])

            for ch in range(2):
                pa = bank()
                proj_fm(ch, pa)
                act(ftmp[0], ps[pa][:, 0:T], AF.Identity, ["pp"], [pst(pa), "ftmp0"], bias=col(B_IN + ch))
                pg = bank()
                proj_fm(2 + ch, pg)
                act(ftmp[1], ps[pg][:, 0:T], AF.Sigmoid, ["pp"], [pst(pg), "ftmp1"], bias=col(B_IN + 2 + ch))
                tt("dve", yglu[:, ch, 30:30 + T], ftmp[0], ftmp[1], ALU.mult, ["ftmp0", "ftmp1"], [("yglu", ch)])
                if i < NPE_CONV:
                    act(yglu_bf[:, ch, 30:30 + T], yglu[:, ch, 30:30 + T], AF.Identity, [("yglu", ch)], [("yglub", ch)])
            if i < NPE_CONV:
                for ch in range(2):
                    pc = bank()
                    for k in range(31):
                        mm(ps[pc][:, 0:T], dg[:, ch * 31 + k, :], yglu_bf[:, ch, k:k + T], k == 0, k == 30,
                           ["dg", ("yglub", ch)], [pst(pc)])
                    act(cacc[:, ch, 0, :], ps[pc][:, 0:T], AF.Identity, ["pp"], [pst(pc), ("cacc", ch, 0)],
                        bias=col(CAB + ch))
                    cp("pool", yglu[:, ch, 0:30], yglu[:, ch, T:T + 30], [], [("yglu", ch)])
                    if i + 1 < NPE_CONV:
                        cp("pool", yglu_bf[:, ch, 0:30], yglu_bf[:, ch, T:T + 30], [], [("yglub", ch)])
            for k in (range(31) if i >= NPE_CONV else ()):
                for ch in range(2):
                    cpar = k % 2
                    eng = "dve"
                    wk = col(CAW + ch * 31 + k)
                    src = yglu[:, ch, k:k + T]
                    dst = cacc[:, ch, cpar, :]
                    tok = ("cacc", ch, cpar)
                    if k < 2:
                        if k == 0:
                            ts(eng, dst, src, wk, col(CAB + ch), ALU.mult, ALU.add, [("yglu", ch), "pp"], [tok])
                        else:
                            ts(eng, dst, src, wk, None, ALU.mult, None, [("yglu", ch), "pp"], [tok])
                    else:
                        stt(eng, dst, src, wk, dst, ALU.mult, ALU.add, [("yglu", ch), "pp"], [tok])
            for ch in (range(2) if i >= NPE_CONV else ()):
                eng = "dve" if ch == 0 else "pool"
                tt(eng, cacc[:, ch, 0, :], cacc[:, ch, 0, :], cacc[:, ch, 1, :], ALU.add,
                   [("cacc", ch, 1)], [("cacc", ch, 0)])
                cp(eng, yglu[:, ch, 0:30], yglu[:, ch, T:T + 30], [], [("yglu", ch)])
            for ch in range(2):
                y = cacc[:, ch, 0, :]
                ytok = ("cacc", ch, 0)
                cp("pool", ftbf[0], y, [ytok], ["ftbf0"])
                act(ftbf[1], y, AF.Square, [ytok], ["ftbf1"])
                p1 = bank()
                p2 = bank()
                mm(ps[p1][:, 0:T], blk_bf[:], ftbf[0], True, True, ["blk_bf", "ftbf0"], [pst(p1)])
                mm(ps[p2][:, 0:T], blk_bf[:], ftbf[1], True, True, ["blk_bf", "ftbf1"], [pst(p2)])
                ts("dve", ftmp[0], ps[p1][:, 0:T], 1.0 / 64, None, ALU.mult, None, [], [pst(p1), "ftmp0"])
                tt("pool", ftmp[1], ftmp[0], ftmp[0], ALU.mult, ["ftmp0"], ["ftmp1"])
                stt("dve", ftmp[2], ps[p2][:, 0:T], 1.0 / 64, ftmp[1], ALU.mult, ALU.subtract, ["ftmp1"],
                    [pst(p2), "ftmp2"])
                act(ftmp[2], ftmp[2], AF.Ln, ["ftmp2"], ["ftmp2"], bias=EPS)
                act(ftmp[2], ftmp[2], AF.Exp, ["ftmp2"], ["ftmp2"], scale=-0.5)
                tt("pool", ftmp[3], y, ftmp[0], ALU.subtract, [ytok, "ftmp0"], ["ftmp3"])
                tt("dve", ftmp[3], ftmp[3], ftmp[2], ALU.mult, ["ftmp3", "ftmp2"], ["ftmp3"])
                act(ycat[:, ch, :], ftmp[3], AF.Silu, ["ftmp3", "pp"], [YC(ch)], bias=col(GNB + ch),
                    scale=col(GNG + ch))
            for fc in range(4, 12):
                pb = bank()
                proj_fm(fc, pb)
                act(ftmp[4], ps[pb][:, 0:T], AF.Identity, ["pp"], [pst(pb), "ftmp4"], bias=col(B_IN + fc))
                pr = bank()
                mm(ps[pr][:, 0:T], perm, ftmp[4], True, True, ["cm", "ftmp4"], [pst(pr)])
                tt("pool", ftmp[5], ftmp[4], cs_c, ALU.mult, ["ftmp4", "cs_c"], ["ftmp5"])
                tt("dve", ftmp[3], ps[pr][:, 0:T], cs_s, ALU.mult, ["cs_s"], [pst(pr), "ftmp3"])
                if fc < 8:
                    hd = fc - 4
                    tt("pool", qT[:, hd, :], ftmp[5], ftmp[3], ALU.add, ["ftmp5", "ftmp3"], [QT(hd)])
                else:
                    hd = fc - 8
                    tt("pool", kT[:, hd, tsl], ftmp[5], ftmp[3], ALU.add, ["ftmp5", "ftmp3"], [("kT", hd, i)])
            for s in range(NSUB):
                blkno = t0 // 128 + s
                pb = bank()
                for kc in range(NKC):
                    mm(ps[pb][:, 0:512], hb[:, kc, s * 128:(s + 1) * 128], win[:, kc, 1536:2048], kc == 0,
                       kc == NKC - 1, [("win", 3), ("hb", kc)], [pst(pb)])
                tt("dve", Vc[:, blkno, :], ps[pb][:, 0:512], rows[:, 0:512], ALU.add, ["rows"],
                   [pst(pb), ("V", blkno)] + (["dg"] if blkno >= 16 else []))
                pb = bank()
                for kc in range(NKC):
                    mm(ps[pb][:, 0:256], hb[:, kc, s * 128:(s + 1) * 128], win[:, kc, 2304:2560], kc == 0,
                       kc == NKC - 1, [("win", 4), ("hb", kc)], [pst(pb)])
                w0, w1 = vcw
                tt("dve", w0, ps[pb][:, 0:256], rows[:, 512:768], ALU.add, ["rows"], [pst(pb), "vcw0"])
                act(w1, w0, AF.Gelu, ["vcw0"], ["vcw1", "st_a"], accum=st4[:, 4:5])
                ts("dve", st4[:, 5:6], st4[:, 4:5], -1.0 / 256, None, ALU.mult, None, ["st_a"], ["st_b"])
                act(w0, w1, AF.Square, ["vcw1", "st_b"], ["vcw0", "st_c"], bias=st4[:, 5:6], accum=st4[:, 6:7])
                ts("dve", st4[:, 7:8], st4[:, 6:7], 1.0 / 256, EPS, ALU.mult, ALU.add, ["st_c"], ["st_d"])
                tt("pool", st4[:, 7:8], st4[:, 7:8], mhalf[:, 0:1], ALU.pow, ["st_d", "mhalf"], ["st_d"])
                ts("dve", w1, w1, st4[:, 5:6], st4[:, 7:8], ALU.add, ALU.mult, ["vcw1", "st_b", "st_d"], ["vcw1"])
                tt("pool", w1, w1, rows[:, 768:1024], ALU.mult, ["vcw1", "rows"], ["vcw1"])
                tt("dve", vn[:, s, :], w1, rows[:, 1024:1280], ALU.add, ["vcw1", "rows"], [("vn", s)])
            for j in range(2):
                pb = bank()
                proj_fm(16 + j, pb)
                act(uT[:, j, :], ps[pb][:, 0:T], AF.Gelu, ["pp"], [pst(pb), ("uT", j)], bias=col(B_IN + 16 + j))
            for j in range(2):
                for s in range(NSUB):
                    pb = bank()
                    for hh in range(2):
                        hdx = 2 * j + hh
                        mm(ps[pb][hh * 64:(hh + 1) * 64, 0:128], vn[:, s, hdx * 64:(hdx + 1) * 64], wsp[:, hdx, :],
                           True, True, [("vn", s), ("wsp", hdx)], [pst(pb)])
                    tt("dve", ftmp[0][:, 0:128], ps[pb][:, 0:128], bsp[:, j, :], ALU.add, ["bsp"], [pst(pb), "ftmp0"])
                    tt("pool", ycat[:, 6 + j, s * 128:(s + 1) * 128], ftmp[0][:, 0:128], uT[:, j, s * 128:(s + 1) * 128],
                       ALU.mult, ["ftmp0", ("uT", j)], [YC(6 + j)])
            nkb = (t0 + T) // 128
            npair = nkb // 2
            npt = 0
            for hd in range(4):
                def qk(kp, hd=hd):
                    pAB = (sbank(), sbank())
                    for kk in range(2):
                        kb = 2 * kp + kk
                        for m in range(2):
                            mm(ps[pAB[m]][:, kk * T:(kk + 1) * T], kT[m * 64:(m + 1) * 64, hd, kb * 128:(kb + 1) * 128],
                               qT[m * 64:(m + 1) * 64, hd, :], True, True, [("kT", hd, (kb * 128) // T), QT(hd)],
                               [pst(pAB[m])])
                    return pAB
                for kp in range(npair):
                    pAB = qk(kp)
                    for m in range(2):
                        pt = PT[npt % len(PT)]
                        ptok = ("PT", npt % len(PT))
                        npt += 1
                        act(pt[:, :, :], ps[pAB[m]][:, 0:2 * T].rearrange("p (k t) -> p k t", k=2), AF.Exp,
                            [], [pst(pAB[m]), ptok], scale=0.125)
                        for kk in range(2):
                            kb = 2 * kp + kk
                            if kb * 128 >= t0:
                                c0 = kb * 128 - t0
                                tt("pool", pt[:, kk, c0:c0 + 128], pt[:, kk, c0:c0 + 128], tri2_bf[:, 0, :], ALU.mult,
                                   ["tri2_bf"], [ptok])
                        for kk in range(2):
                            kb = 2 * kp + kk
                            c0 = max(0, kb * 128 - t0)
                            mm(ps[m][:, c0:T], Vc[:, kb, hd * 128:(hd + 1) * 128], pt[:, kk, c0:T], kb == 0, False,
                               [("V", kb), ptok], [pst(m)])
                            mm(ps[m][:, T + c0:2 * T], ones_bf[:], pt[:, kk, c0:T], False, kb == nkb - 1,
                               ["ones_bf", ptok], [pst(m)])
                for m in range(2):
                    act(OL[m], ps[m][:, 0:2 * T], AF.Identity, [], [pst(m), "tmp%d" % (2 * m), "tmp%d" % (2 * m + 1)])
                for m in range(2):
                    tk2 = ["tmp%d" % (2 * m), "tmp%d" % (2 * m + 1)]
                    P.op("dve", lambda e, o=OL[m][:, T:2 * T]: e.reciprocal(out=o, in_=o), [], tk2, cost=0.6)
                    tt("dve", OL[m][:, 0:T], OL[m][:, 0:T], OL[m][:, T:2 * T], ALU.mult, [], tk2)
                stt("dve", OL[0][:, 0:T], OL[1][:, 0:T], dcol(NEGLAM), OL[0][:, 0:T], ALU.mult, ALU.add,
                    ["tmp2", "tmp3", ("dv", l, "neglam")], ["tmp0", "tmp1"])
                act(tbf[2], OL[0][:, 0:T], AF.Square, ["tmp0", "tmp1"], ["tbf2"])
                pb = sbank()
                mm(ps[pb][:, 0:T], ones_bf[:], tbf[2], True, True, ["ones_bf", "tbf2"], [pst(pb)])
                act(tmp[4], ps[pb][:, 0:T], AF.Ln, [], [pst(pb), "tmp4"], bias=EPS, scale=1.0 / 128)
                act(tmp[4], tmp[4], AF.Exp, ["tmp4"], ["tmp4"], scale=-0.5)
                stt("dve", ycat[:, 2 + hd, :], OL[0][:, 0:T], dcol(SUBG2), tmp[4], ALU.mult, ALU.mult,
                    ["tmp0", "tmp1", "tmp4", (("dv", l), 7)], [YC(2 + hd)])
            for oc in range(NKC):
                pb = bbank()
                for kc in range(NKC):
                    mm(ps[pb][:, 0:T], wout[:, kc, oc * 128:(oc + 1) * 128], ycat[:, kc, :], kc == 0, kc == NKC - 1,
                       [("wout", kc), YC(kc)], [pst(pb)])
                tk = "tmp%d" % (oc % 2)
                act(tmp[oc % 2], ps[pb][:, 0:T], AF.Identity, [(("dv", l), 2), (("dv", l), 5)], [pst(pb), tk],
                    bias=dcol(BG1 + oc), scale=dcol(G1P + oc))
                stt("dve", xt[:, oc, :], xt[:, oc, :], ALPHA, tmp[oc % 2], ALU.mult, ALU.add, [tk], [XT(oc)])
            layer_norm_tile(locals(), l, 0, XT)
            dma("sp", x_mid.rearrange("(c p) t -> p c t", p=128)[:, :, tsl], xt, [XT(c) for c in range(NKC)],
                [("xd", l, 1, i)])
        if stop_phase == (l, 0):
            break
        P.fence()
        A.reset()
        wup = A.take([128, NKC, 2 * FF], BF16)
        wdn = A.take([128, NHC, D], BF16)
        xt2 = [A.take([128, NKC, T], F32) for _ in range(2)]
        h2 = [A.take([128, NKC, T + 2], BF16) for _ in range(2)]
        ug = [A.take([128, T + 2], F32) for _ in range(2)]
        uv = [A.take([128, T + 2], F32) for _ in range(2)]
        cg = [A.take([128, T], F32) for _ in range(2)]
        cv = [A.take([128, T], F32) for _ in range(2)]
        sg = [A.take([128, T], F32) for _ in range(2)]
        hmid = A.take([128, NHC, T], BF16)
        tmp = [A.take([128, T], F32) for _ in range(6)]
        tbf = [A.take([128, T], BF16) for _ in range(4)]
        mhalf = A.take([128, T], F32)
        P.op("pool", lambda e, a=mhalf: e.memset(a, -0.5), [], ["mhalf"])
        for j in (0, 5, 6, 1, 7, 2, 8, 3, 9, 4, 10):
            dma("pool", wup[:, :, j * 512:(j + 1) * 512],
                w_up[l].rearrange("(kc p) n -> p kc n", p=128)[:, :, j * 512:(j + 1) * 512], [], [("wup", j)])
        for j in range(11):
            dma("pool", wdn[:, 2 * j:2 * j + 2, :],
                w_dn[l, j * 256:(j + 1) * 256, :].rearrange("(kc p) n -> p kc n", p=128), [], [("wdn", j)])
        P.op("pool", lambda e, a=h2[0][:, :, 0:2]: e.memset(a, 0.0), [], [("h2h", 0)])
        for i in range(n_tiles):
            t0 = i * T
            tsl = slice(t0, t0 + T)
            par = i % 2
            xt = xt2[par]
            XT = lambda c, par=par: ("xt", par, c)
            hcur = h2[i % 2]
            hnext = h2[(i + 1) % 2]
            dma("sp", xt, xa.rearrange("(c p) t -> p c t", p=128)[:, :, tsl], [("xd", l, 1, i)],
                [XT(c) for c in range(NKC)])
            for c in range(NKC):
                ts(ve(), hcur[:, c, 2:2 + T], xt[:, c, :], dcol(SC2P + c), dcol(24 + c), ALU.mult, ALU.add,
                   [XT(c), ("dv", l), (("dv", l), 3)], [("h2", i % 2, c)])
            cp("pool", hnext[:, :, 0:2], hcur[:, :, T:T + 2], [("h2", i % 2, c) for c in range(NKC)],
               [("h2h", (i + 1) % 2)])
            for c in range(NHC):
                b2 = c % 2
                pgk = fbank()
                for kc in range(NKC):
                    mm(ps[pgk][:, 0:T + 2], wup[:, kc, c * 128:(c + 1) * 128], hcur[:, kc, :], kc == 0, kc == NKC - 1,
                       [("wup", c // 4), ("h2", i % 2, kc), ("h2h", i % 2)], [pst(pgk)])
                act(ug[b2], ps[pgk][:, 0:T + 2], AF.Identity, ["pp"], [pst(pgk), ("ug", b2)], bias=col(BUP + c))
                pvk = fbank()
                cv_ = NHC + c
                for kc in range(NKC):
                    mm(ps[pvk][:, 0:T + 2], wup[:, kc, cv_ * 128:(cv_ + 1) * 128], hcur[:, kc, :], kc == 0,
                       kc == NKC - 1, [("wup", cv_ // 4), ("h2", i % 2, kc), ("h2h", i % 2)], [pst(pvk)])
                act(uv[b2], ps[pvk][:, 0:T + 2], AF.Identity, ["pp"], [pst(pvk), ("uv", b2)], bias=col(BUP + cv_))
                if i == 0:
                    P.op("pool", lambda e, a=ug[b2][:, 0:2]: e.memset(a, 0.0), [], [("ug", b2)])
                    P.op("pool", lambda e, a=uv[b2][:, 0:2]: e.memset(a, 0.0), [], [("uv", b2)])
                act(cg[b2], ps[pgk][:, 2:2 + T], AF.Identity, ["pp", (("dv", l), 8)], [pst(pgk), ("cg", b2)],
                    bias=dcol(CF2 + c), scale=col(CFW + c * 3 + 2))
                ts("pool", cv[b2], uv[b2][:, 2:2 + T], col(CFW + cv_ * 3 + 2), col(CFB + cv_), ALU.mult, ALU.add,
                   [("uv", b2), "pp"], [("cv", b2)])
                for k in (1, 0):
                    for (u_, c_, tok_u, tok_c, cc) in ((ug[b2], cg[b2], ("ug", b2), ("cg", b2), c),
                                                       (uv[b2], cv[b2], ("uv", b2), ("cv", b2), cv_)):
                        stt("dve", c_, u_[:, k:k + T], col(CFW + cc * 3 + k), c_, ALU.mult, ALU.add, [tok_u, "pp"], [tok_c])
                act(sg[b2], cg[b2], AF.Silu, [("cg", b2)], [("sg", b2)])
                tt("pool", hmid[:, c, :], sg[b2], cv[b2], ALU.mult, [("sg", b2), ("cv", b2)], [("hmid", c)])
            for oc in range(NKC):
                pb = fbank()
                for kc in range(NHC):
                    mm(ps[pb][:, 0:T], wdn[:, kc, oc * 128:(oc + 1) * 128], hmid[:, kc, :], kc == 0, kc == NHC - 1,
                       [("wdn", kc // 2), ("hmid", kc)], [pst(pb)])
                tk = "tmp%d" % (oc % 2)
                act(tmp[oc % 2], ps[pb][:, 0:T], AF.Identity, [(("dv", l), 4), (("dv", l), 6)], [pst(pb), tk],
                    bias=dcol(BG2 + oc), scale=dcol(G2P + oc))
                stt(ve(), xt[:, oc, :], xt[:, oc, :], ALPHA, tmp[oc % 2], ALU.mult, ALU.add, [tk], [XT(oc)])
            layer_norm_tile(locals(), l, 1, XT)
            dma("sp", x_out.rearrange("(c p) t -> p c t", p=128)[:, :, tsl], xt, [XT(c) for c in range(NKC)],
                [("xd", l + 1, 0, i)])
        if stop_phase == (l, 1):
            break
    P.emit(nc, es)
    es.close()
    return nc


def layer_norm_tile(env, l, which, XT):
    xt, ps, tmp, tbf, pp = env["xt"], env["ps"], env["tmp"], env["tbf"], env["pp"]
    ones_bf = env["ones_bf"]
    mm, act, ts, tt, stt, cp = env["mm"], env["act"], env["ts"], env["tt"], env["stt"], env["cp"]
    pst = env["pst"]
    gcol = LNG1 if which == 0 else LNG2
    bcol = LNB1 if which == 0 else LNB2
    p1, p2 = 0, 1
    for c in range(NKC):
        b = c % 2
        cp("dve" if c % 2 else "pool", tbf[b], xt[:, c, :], [XT(c)], ["tbf%d" % b])
        act(tbf[2 + b], xt[:, c, :], AF.Square, [XT(c)], ["tbf%d" % (2 + b)])
        mm(ps[p1][:, 0:T], ones_bf[:], tbf[b], c == 0, c == NKC - 1, ["ones_bf", "tbf%d" % b], [pst(p1)])
        mm(ps[p2][:, 0:T], ones_bf[:], tbf[2 + b], c == 0, c == NKC - 1, ["ones_bf", "tbf%d" % (2 + b)], [pst(p2)])
    ts("dve", tmp[2], ps[p1][:, 0:T], 1.0 / D, None, ALU.mult, None, [], [pst(p1), "tmp2"])
    tt("pool", tmp[3], tmp[2], tmp[2], ALU.mult, ["tmp2"], ["tmp3"])
    stt("dve", tmp[4], ps[p2][:, 0:T], 1.0 / D, tmp[3], ALU.mult, ALU.subtract, ["tmp3"], [pst(p2), "tmp4"])
    act(tmp[4], tmp[4], AF.Ln, ["tmp4"], ["tmp4"], bias=EPS)
    act(tmp[4], tmp[4], AF.Exp, ["tmp4"], ["tmp4"], scale=-0.5)
    for c in range(NKC):
        e1 = "pool" if c % 2 else "dve"
        e2 = "dve" if c % 2 else "pool"
        tt(e1, xt[:, c, :], xt[:, c, :], tmp[2], ALU.subtract, ["tmp2"], [XT(c)])
        tt(e2, xt[:, c, :], xt[:, c, :], tmp[4], ALU.mult, ["tmp4"], [XT(c)])
        ts(e1, xt[:, c, :], xt[:, c, :], pp[:, l, gcol + c:gcol + c + 1], pp[:, l, bcol + c:bcol + c + 1],
           ALU.mult, ALU.add, ["pp"], [XT(c)])


def _consts():
    cm = np.zeros((128, NCM), np.float32)
    j = np.arange(128)
    src = 64 * (j // 64) + ((j % 64) + 32) % 64
    cm[src, C_PERM + j] = 1.0
    cm[:, C_BLK:C_BLK + 128] = (j[:, None] // 64 == j[None, :] // 64)
    cm[:, C_ONES:C_ONES + 128] = 1.0
    cm[:, C_TRI:C_TRI + 128] = (j[:, None] <= j[None, :])
    inv_freq = (1.0 / (np.float32(10000.0) ** (np.arange(0, 64, 2, dtype=np.float32) / np.float32(64)))).astype(np.float32)
    cm[:, C_INVF] = inv_freq[(j % 64) % 32]
    cm[:, C_SGN] = np.where((j % 64) < 32, -1.0, 1.0)
    cm[j, C_ID + j] = 1.0
    return cm


def _cols(v):
    v = np.asarray(v, np.float32)
    return v.reshape(-1, 128).T


def _pack(inp):
    pp = np.zeros((128, L, NP), np.float32)
    rows = np.zeros((L, 128, 1280), np.float32)
    for l in range(L):
        pp[:, l, B_IN:B_IN + 20] = _cols(inp["b_in"][l])
        caw = np.asarray(inp["conv_a_w"][l], np.float32)
        for ch in range(2):
            pp[:, l, CAW + ch * 31:CAW + (ch + 1) * 31] = caw[:, ch * 128:(ch + 1) * 128].T
        pp[:, l, CAB:CAB + 2] = _cols(inp["conv_a_b"][l])
        pp[:, l, GNG:GNG + 2] = _cols(inp["gn_a_g"][l])
        pp[:, l, GNB:GNB + 2] = _cols(inp["gn_a_b"][l])
        pp[:, l, SUBG] = np.asarray(inp["subln_g"][l], np.float32)
        pp[:, l, BOUT:BOUT + 8] = _cols(inp["b_out"][l])
        pp[:, l, LNG1:LNG1 + 8] = _cols(inp["ln_g"][l, 0])
        pp[:, l, LNB1:LNB1 + 8] = _cols(inp["ln_b"][l, 0])
        pp[:, l, BUP:BUP + 44] = _cols(inp["b_up"][l])
        cfw = np.asarray(inp["conv_f_w"][l], np.float32)
        pp[:, l, CFW:CFW + 132] = cfw.reshape(3, 44, 128).transpose(2, 1, 0).reshape(128, 132)
        pp[:, l, CFB:CFB + 44] = _cols(inp["conv_f_b"][l])
        pp[:, l, BDN:BDN + 8] = _cols(inp["b_down"][l])
        pp[:, l, LNG2:LNG2 + 8] = _cols(inp["ln_g"][l, 1])
        pp[:, l, LNB2:LNB2 + 8] = _cols(inp["ln_b"][l, 1])
        pp[:, l, BADA:BADA + 48] = _cols(inp["b_ada"][l])
        b_in = np.asarray(inp["b_in"][l], np.float32)
        rows[l, :, 0:512] = b_in[None, 1536:2048]
        rows[l, :, 512:768] = b_in[None, 2304:2560]
        rows[l, :, 768:1024] = np.asarray(inp["ln_c_g"][l], np.float32)[None]
        rows[l, :, 1024:1280] = np.asarray(inp["ln_c_b"][l], np.float32)[None]
    lamp = np.ascontiguousarray(np.broadcast_to(np.asarray(inp["lam_p"], np.float32).reshape(L, 1, 256), (L, 128, 256)))
    wsp = np.ascontiguousarray(np.asarray(inp["w_sp"], np.float32).transpose(0, 3, 1, 2))
    b_sp = np.asarray(inp["b_sp"], np.float32)
    bsp = np.ascontiguousarray(np.repeat(b_sp.reshape(L, 2, 2, 1, 128), 64, axis=3).reshape(L, 2, 128, 128)
                               .transpose(0, 2, 1, 3))
    return pp, rows, lamp, wsp, bsp


def _run(inputs, n_layers=L, n_tiles=S // T, stop_phase=None, cores=8, trace=False):
    inp = {k: np.asarray(v) for k, v in inputs.items()}
    pp, rows, lamp, wsp, bsp = _pack(inp)
    cm = _consts()
    shared = {
        "cm": cm, "pp": pp, "rows": rows, "lamp": lamp, "wspT": wsp, "bsp": bsp,
        "w_ada": np.ascontiguousarray(inp["w_ada"], np.float32), "w_in": np.ascontiguousarray(inp["w_in"], np.float32),
        "w_out": np.ascontiguousarray(inp["w_out"], np.float32), "w_up": np.ascontiguousarray(inp["w_up"], np.float32),
        "w_down": np.ascontiguousarray(inp["w_down"], np.float32),
    }
    in_maps = []
    for b in range(cores):
        m = dict(shared)
        m["xT"] = np.ascontiguousarray(inp["x"][b].T.astype(np.float32, copy=False))
        m["pos"] = np.ascontiguousarray(inp["positions"][b].reshape(1, S).astype(np.int32, copy=False))
        m["cvec"] = np.ascontiguousarray(_cols(inp["c"][b]))
        in_maps.append(m)
    nc = build(n_layers, n_tiles, stop_phase)
    res = run_bass_kernel_spmd(nc, in_maps, core_ids=list(range(cores)), trace=trace)
    outs = [np.asarray(r["yT"]) for r in res.results]
    return outs, res


def kernel(**inputs):
    outs, _ = _run(inputs)
    return np.stack([o.T for o in outs], axis=0).astype(np.float32)
```

```python
import math
from contextlib import ExitStack

import numpy as np
import concourse.bass as bass
import concourse.mybir as mybir
from concourse.bass_utils import run_bass_kernel_spmd

F32 = mybir.dt.float32
BF16 = mybir.dt.bfloat16
I32 = mybir.dt.int32
ALU = mybir.AluOpType
AF = mybir.ActivationFunctionType

D = 1024
S = 4096
L = 4
T = 256
NSUB = T // 128
INW = 2560
FF = 2816
NKC = 8
NFC = 2 * FF // 128
NHC = FF // 128
ALPHA = (2 * L) ** 0.25
EPS = 1e-5

B_IN, CAW, CAB, GNG, GNB, SUBG = 0, 20, 82, 84, 86, 88
BOUT, LNG1, LNB1 = 89, 97, 105
BUP, CFW, CFB = 113, 157, 289
BDN, LNG2, LNB2 = 333, 341, 349
BADA = 357
NP = 405
SC1P, G1P, SC2P, G2P, BG1, BG2, NEGLAM, SUBG2, DTMP = 48, 56, 64, 72, 80, 88, 96, 97, 98
CF2 = 104
ND = 148
C_PERM, C_BLK, C_ONES, C_TRI, C_INVF, C_SGN, C_ID = 0, 128, 256, 384, 512, 513, 514
NCM = 642
NPE_CONV = 8

ENGS = ("pe", "act", "dve", "pool", "sp")
SAME_ENG_DIST = 10 ** 9
N_DMA_SEMS = 12


class Op:
    __slots__ = ("eng", "fn", "r", "w", "dma", "eidx", "deps", "need_inc", "inc_val", "dsem",
                 "dval", "waits", "epoch", "fence", "skip", "cost", "lat")


class Prog:
    def __init__(self):
        import os
        self.ops = []
        self.epoch = 0
        self.max_ops = int(os.environ["KMAXOPS"]) if "KMAXOPS" in os.environ else None
        self.window = int(os.environ.get("KWINDOW", "3000"))
        self.do_sched = os.environ.get("KSCHED", "1") == "1"

    def op(self, eng, fn, r=(), w=(), cost=0.3):
        if self.max_ops is not None and len(self.ops) >= self.max_ops and fn is not None:
            fn = None
            o = Op()
            o.eng, o.fn, o.r, o.w = eng, None, (), ()
            o.dma = False
            o.need_inc = False
            o.epoch = self.epoch
            o.fence = False
            o.skip = True
            self.ops.append(o)
            return o
        o = Op()
        o.skip = False
        o.cost = cost
        o.lat = 0.0
        o.eng, o.fn, o.r, o.w = eng, fn, tuple(r), tuple(w)
        o.dma = False
        o.need_inc = False
        o.epoch = self.epoch
        o.fence = False
        self.ops.append(o)
        return o

    def dma(self, eng, fn, r=(), w=(), nbytes=0):
        o = self.op(eng, fn, r, w, cost=0.15)
        o.dma = True
        o.lat = 2.5 + nbytes / 120e3
        return o

    def schedule(self):
        out = []
        seg = []
        for o in self.ops:
            if o.fence or o.skip:
                if seg:
                    out.extend(self._sched_seg(seg))
                    seg = []
                out.append(o)
            else:
                seg.append(o)
        if seg:
            out.extend(self._sched_seg(seg))
        self.ops = out

    def _sched_seg(self, ops):
        n = len(ops)
        idx = {id(o): i for i, o in enumerate(ops)}
        preds = [set() for _ in range(n)]
        last_w = {}
        readers = {}
        for i, o in enumerate(ops):
            for t in o.r:
                lw = last_w.get(t)
                if lw is not None:
                    preds[i].add(lw)
            for t in o.w:
                lw = last_w.get(t)
                if lw is not None:
                    preds[i].add(lw)
                rl = readers.get(t)
                if rl:
                    preds[i].update(rl)
            preds[i].discard(i)
            for t in o.r:
                readers.setdefault(t, []).append(i)
            for t in o.w:
                last_w[t] = i
                readers[t] = []
        succs = [[] for _ in range(n)]
        indeg = [0] * n
        for i in range(n):
            indeg[i] = len(preds[i])
            for p in preds[i]:
                succs[p].append(i)
        finish = [0.0] * n
        ready = [0.0] * n
        eng_free = {e: 0.0 for e in ENGS}
        avail = {e: [] for e in ENGS}
        for i in range(n):
            if indeg[i] == 0:
                avail[ops[i].eng].append(i)
        order = []
        done = 0
        lo = 0
        sched = [False] * n
        W = self.window
        SYNC = 0.25
        while done < n:
            best = None
            lim = lo + W
            for e in ENGS:
                ef = eng_free[e]
                for i in avail[e]:
                    if i >= lim:
                        continue
                    st = ready[i] if ready[i] > ef else ef
                    key = (st, i)
                    if best is None or key < best:
                        best = key
            st, i = best
            o = ops[i]
            avail[o.eng].remove(i)
            eng_free[o.eng] = st + o.cost
            finish[i] = st + o.cost + o.lat
            sched[i] = True
            order.append(o)
            done += 1
            while lo < n and sched[lo]:
                lo += 1
            for j in succs[i]:
                indeg[j] -= 1
                r = finish[i] + (SYNC if ops[j].eng != o.eng or o.dma else 0.0)
                if r > ready[j]:
                    ready[j] = r
                if indeg[j] == 0:
                    avail[ops[j].eng].append(j)
        self.est_time = getattr(self, "est_time", 0.0) + max(finish) if n else 0.0
        return order

    def fence(self):
        for e in ENGS:
            o = self.op(e, None)
            o.fence = True
            o.skip = False
        self.epoch += 1

    def analyze(self):
        last_w, rd_c, rd_d = {}, {}, {}
        ecount = {e: 0 for e in ENGS}
        dcount = {e: 0 for e in ENGS}
        slot_last = {}
        seen = {e: {p: -1 for p in ENGS} for e in ENGS}
        dseen = {e: {} for e in ENGS}
        cnt = {e: 0 for e in ENGS}
        pending = []
        cur_epoch = 0
        for o in self.ops:
            if o.fence or o.skip:
                o.waits = []
                continue
            if o.epoch != cur_epoch:
                cur_epoch = o.epoch
                last_w, rd_c, rd_d = {}, {}, {}
                seen = {e: {p: -1 for p in ENGS} for e in ENGS}
                ecount = {e: 0 for e in ENGS}
            o.eidx = ecount[o.eng]
            ecount[o.eng] += 1
            deps = set()
            for t in o.r:
                lw = last_w.get(t)
                if lw is not None:
                    deps.add(lw)
            for t in o.w:
                lw = last_w.get(t)
                if lw is not None:
                    deps.add(lw)
                d = rd_c.get(t)
                if d:
                    deps.update(d.values())
                l = rd_d.get(t)
                if l:
                    deps.update(l)
            if o.dma:
                k = dcount[o.eng] % N_DMA_SEMS
                dcount[o.eng] += 1
                key = (o.eng, k)
                prev = slot_last.get(key)
                if prev is not None:
                    deps.add(prev)
                    o.dval = prev.dval + 16
                else:
                    o.dval = 16
                o.dsem = key
                slot_last[key] = o
            deps.discard(o)
            for t in o.r:
                if o.dma:
                    rd_d.setdefault(t, []).append(o)
                else:
                    rd_c.setdefault(t, {})[o.eng] = o
            for t in o.w:
                last_w[t] = o
                rd_c[t] = {}
                rd_d[t] = []
            best = {}
            dm = {}
            for d in deps:
                if d.dma:
                    if d.dval > dseen[o.eng].get(d.dsem, 0):
                        if d.dsem not in dm or d.dval > dm[d.dsem].dval:
                            dm[d.dsem] = d
                else:
                    if d.eng == o.eng and not o.dma:
                        if o.eng == "pe":
                            continue
                        if o.eidx - d.eidx > SAME_ENG_DIST:
                            continue
                    b = best.get(d.eng)
                    if b is None or d.eidx > b.eidx:
                        best[d.eng] = d
            waits = []
            for d in dm.values():
                dseen[o.eng][d.dsem] = d.dval
                waits.append(d)
            for p, d in best.items():
                if d.eidx <= seen[o.eng][p]:
                    continue
                seen[o.eng][p] = d.eidx
                d.need_inc = True
                waits.append(d)
            o.waits = waits
        for o in self.ops:
            if o.fence:
                cnt = {e: 0 for e in ENGS}
                continue
            if o.skip:
                continue
            if (not o.dma) and o.need_inc:
                cnt[o.eng] += 1
                o.inc_val = cnt[o.eng]
        self.slot_last = slot_last

    def emit(self, nc, es):
        if self.do_sched:
            self.schedule()
        self.analyze()
        nep = self.epoch + 1
        csem = {(e, k): es.enter_context(nc.semaphore("c_%s_%d" % (e, k))) for e in ENGS for k in range(nep)}
        fsem = es.enter_context(nc.semaphore("fence"))
        dsem = {key: es.enter_context(nc.semaphore("d_%s_%d" % key)) for key in self.slot_last}
        by_eng = {e: [] for e in ENGS}
        for o in self.ops:
            by_eng[o.eng].append(o)

        def run(ename, eng):
            issued = {}
            nf = 0
            for o in by_eng[ename]:
                if o.fence:
                    for key, val in issued.items():
                        eng.wait_ge(dsem[key], val)
                    nf += 1
                    eng.drain().then_inc(fsem, 1)
                    eng.wait_ge(fsem, len(ENGS) * nf)
                    continue
                if o.skip:
                    continue
                for d in o.waits:
                    if d.dma:
                        eng.wait_ge(dsem[d.dsem], d.dval)
                    else:
                        eng.wait_ge(csem[(d.eng, d.epoch)], d.inc_val)
                ins = o.fn(eng)
                if o.dma:
                    ins.then_inc(dsem[o.dsem], 16)
                    issued[o.dsem] = o.dval
                elif o.need_inc:
                    ins.then_inc(csem[(ename, o.epoch)], 1)
            for key, val in issued.items():
                eng.wait_ge(dsem[key], val)

        with nc.Block() as block:
            @block.tensor
            def _(e):
                run("pe", e)

            @block.scalar
            def _(e):
                run("act", e)

            @block.vector
            def _(e):
                run("dve", e)

            @block.gpsimd
            def _(e):
                run("pool", e)

            @block.sync
            def _(e):
                run("sp", e)


class Arena:
    def __init__(self, ap_f32, nbytes):
        self.ap = ap_f32
        self.nbytes = nbytes
        self.off = 0

    def reset(self, off=0):
        self.off = off

    def take(self, shape, dt):
        esz = 4 if dt in (F32, I32) else 2
        n = 1
        for s in shape[1:]:
            n *= s
        nb = (n * esz + 31) // 32 * 32
        a = self.off
        assert a + nb <= self.nbytes, ("arena overflow", a, nb, self.nbytes)
        self.off += nb
        v = self.ap[:, a // 4:(a + nb) // 4]
        if dt != F32:
            v = v.bitcast(dt)
        v = v[:, 0:n]
        if len(shape) == 3:
            v = v.rearrange("p (a b) -> p a b", a=shape[1])
        elif len(shape) == 4:
            v = v.rearrange("p (a b c) -> p a b c", a=shape[1], b=shape[2])
        return v


def build(n_layers=L, n_tiles=S // T, stop_phase=None):
    nc = bass.Bass("TRN2", target_bir_lowering=False)
    dt_in = lambda name, shape, dt=F32: nc.dram_tensor(name, shape, dt, kind="ExternalInput").ap()
    xT = dt_in("xT", [D, S])
    yT = nc.dram_tensor("yT", [D, S], F32, kind="ExternalOutput").ap()
    xa = nc.dram_tensor("xa", [D, S], F32).ap()
    xb = nc.dram_tensor("xb", [D, S], F32).ap()
    cst = nc.dram_tensor("cst", [2, 128, S], F32).ap()
    pos = dt_in("pos", [1, S], I32)
    cvec = dt_in("cvec", [128, NKC])
    cm_d = dt_in("cm", [128, NCM])
    pp_d = dt_in("pp", [128, L, NP])
    rows_d = dt_in("rows", [L, 128, 1280])
    lamp_d = dt_in("lamp", [L, 128, 256])
    wsp_d = dt_in("wspT", [L, 128, 4, 128])
    bsp_d = dt_in("bsp", [L, 128, 2, 128])
    w_ada = dt_in("w_ada", [L, D, 6 * D])
    w_in = dt_in("w_in", [L, D, INW])
    w_out = dt_in("w_out", [L, D, D])
    w_up = dt_in("w_up", [L, D, 2 * FF])
    w_dn = dt_in("w_down", [L, FF, D])

    P = Prog()
    es = ExitStack()
    es.enter_context(nc.allow_low_precision("bf16 matmul operands, fp32 PSUM accumulation"))
    sb = lambda name, shape, dt: es.enter_context(nc.sbuf_tensor(name, shape, dt))
    cm = sb("cm_sb", [128, NCM], F32)
    pp = sb("pp_sb", [128, L, NP], F32)
    dv = sb("dv", [128, L, ND], F32)
    ones_bf = sb("ones_bf", [128, 128], BF16)
    blk_bf = sb("blk_bf", [128, 128], BF16)
    tri2_bf = sb("tri2_bf", [128, 2, 128], BF16)
    cact = sb("cact", [128, NKC], BF16)
    ARENA_BYTES = 195 * 1024
    arena_t = sb("arena", [128, ARENA_BYTES // 4], F32)
    A = Arena(arena_t, ARENA_BYTES)
    ps = [es.enter_context(nc.psum_tensor("ps%d" % i, [128, 512], F32)) for i in range(8)]
    pst = lambda b: ("ps", b)
    perm = cm[:, C_PERM:C_PERM + 128]
    tri_f = cm[:, C_TRI:C_TRI + 128]
    invf = cm[:, C_INVF:C_INVF + 1]
    sgn = cm[:, C_SGN:C_SGN + 1]

    rot = {"n": 0, "s": 0, "b": 0, "f": 0}

    def bank():
        rot["n"] += 1
        return 6 + rot["n"] % 2

    def sbank():
        rot["s"] += 1
        return 2 + rot["s"] % 4

    def bbank():
        rot["b"] += 1
        return rot["b"] % 2

    def fbank():
        rot["f"] += 1
        return 2 + rot["f"] % 6

    ew = {"n": 0}

    def ve():
        ew["n"] += 1
        return "dve" if ew["n"] % 2 else "pool"

    def fsz(ap):
        n = 1
        for d in ap.shape[1:]:
            n *= d
        return n

    def mm(out, lhsT, rhs, start, stop, r, w):
        c = 0.02 + fsz(out) / 2400.0
        if rhs.dtype == F32:
            c *= 4
        P.op("pe", lambda e: e.matmul(out, lhsT, rhs, start=start, stop=stop), r, w, cost=c)

    def act(out, in_, func, r, w, bias=None, scale=None, accum=None):
        kw = {}
        if bias is not None:
            kw["bias"] = bias
        if scale is not None:
            kw["scale"] = scale
        if accum is not None:
            kw["accum_out"] = accum
        P.op("act", lambda e: e.activation(out=out, in_=in_, func=func, **kw), r, w, cost=(fsz(out) + 335) / 1200.0)

    def vcost(eng, out):
        c = (fsz(out) + 151) / 960.0
        return c * 2.0 if eng == "pool" else c

    def ts(eng, out, in0, s1, s2, op0, op1, r, w):
        if s2 is None:
            P.op(eng, lambda e: e.tensor_scalar(out=out, in0=in0, scalar1=s1, scalar2=None, op0=op0), r, w,
                 cost=vcost(eng, out))
        else:
            P.op(eng, lambda e: e.tensor_scalar(out=out, in0=in0, scalar1=s1, scalar2=s2, op0=op0, op1=op1), r, w,
                 cost=vcost(eng, out))

    def tt(eng, out, in0, in1, op, r, w):
        P.op(eng, lambda e: e.tensor_tensor(out=out, in0=in0, in1=in1, op=op), r, w, cost=vcost(eng, out))

    def stt(eng, out, in0, scalar, in1, op0, op1, r, w):
        eng = "dve"
        P.op(eng, lambda e: e.scalar_tensor_tensor(out=out, in0=in0, scalar=scalar, in1=in1, op0=op0, op1=op1), r, w,
             cost=vcost(eng, out))

    def cp(eng, out, in_, r, w):
        P.op(eng, lambda e: e.tensor_copy(out=out, in_=in_), r, w, cost=vcost(eng, out))

    def dma(q, out, in_, r, w):
        nb = 128 * fsz(out) * (4 if out.dtype in (F32, I32) else 2)
        P.dma(q, lambda e: e.dma_start(out=out, in_=in_), r, w, nbytes=nb)

    dma("sp", cm[:], cm_d, [], ["cm"])
    dma("sp", pp[:], pp_d, [], ["pp"])
    cp("dve", ones_bf[:], cm[:, C_ONES:C_ONES + 128], ["cm"], ["ones_bf"])
    cp("dve", blk_bf[:], cm[:, C_BLK:C_BLK + 128], ["cm"], ["blk_bf"])
    cp("dve", tri2_bf[:, 0, :], tri_f, ["cm"], ["tri2_bf"])
    cp("dve", tri2_bf[:, 1, :], tri_f, ["cm"], ["tri2_bf"])
    A.reset()
    posi = A.take([128, S], I32)
    CB = 1024
    tA = A.take([128, CB], F32)
    tF = A.take([128, CB], F32)
    tN = A.take([128, CB], F32)
    tNi = A.take([128, CB], I32)
    tR = A.take([128, CB], F32)
    tS = A.take([128, CB], F32)
    tC = A.take([128, CB], F32)
    tO1 = A.take([128, CB], F32)
    tO2 = A.take([128, CB], F32)
    cvs = A.take([128, NKC], F32)
    wst = [A.take([128, NKC, 512], BF16) for _ in range(2)]
    dma("sp", posi, pos.broadcast_to([128, S]), [], ["posi"])
    INV2PI = 1.0 / (2.0 * math.pi)
    C1 = 6.28125
    C2 = 2.0 * math.pi - C1
    for cb in range(S // CB):
        sl = slice(cb * CB, (cb + 1) * CB)
        cp("dve", tA, posi[:, sl], ["posi"], ["tA"])
        ts("dve", tA, tA, invf, None, ALU.mult, None, ["tA", "cm"], ["tA"])
        ts("dve", tF, tA, INV2PI, None, ALU.mult, None, ["tA"], ["tF"])
        cp("dve", tNi, tF, ["tF"], ["tNi"])
        cp("dve", tN, tNi, ["tNi"], ["tN"])
        stt("dve", tR, tN, -C1, tA, ALU.mult, ALU.add, ["tN", "tA"], ["tR"])
        stt("dve", tR, tN, -C2, tR, ALU.mult, ALU.add, ["tN", "tR"], ["tR"])
        ts("dve", tS, tR, -math.pi, math.pi, ALU.max, ALU.min, ["tR"], ["tS"])
        act(tO2, tS, AF.Sin, ["tS", "cm"], ["tO2"], scale=sgn)
        ts("dve", tC, tR, math.pi / 2, None, ALU.add, None, ["tR"], ["tC"])
        ts("dve", tF, tC, math.pi, -2.0 * math.pi, ALU.is_gt, ALU.mult, ["tC"], ["tF"])
        tt("dve", tC, tC, tF, ALU.add, ["tC", "tF"], ["tC"])
        ts("dve", tC, tC, -math.pi, math.pi, ALU.max, ALU.min, ["tC"], ["tC"])
        act(tO1, tC, AF.Sin, ["tC"], ["tO1"])
        dma("sp", cst[0, :, sl], tO1, ["tO1"], [("cst", cb)])
        dma("sp", cst[1, :, sl], tO2, ["tO2"], [("cst", cb)])
    dma("sp", cvs, cvec, [], ["cvs"])
    act(cact[:], cvs, AF.Silu, ["cvs"], ["cact"])
    nblk = 0
    for l in range(n_layers):
        pb = l % 4
        for j in range(12):
            wb = wst[nblk % 2]
            wtok = ("wst", nblk % 2)
            nblk += 1
            dma("pool", wb, w_ada[l].rearrange("(kc p) n -> p kc n", p=128)[:, :, j * 512:(j + 1) * 512],
                [], [wtok])
            for fc in range(4):
                col = j * 4 + fc
                for kc in range(NKC):
                    mm(ps[pb][:, col:col + 1], wb[:, kc, fc * 128:(fc + 1) * 128], cact[:, kc:kc + 1],
                       kc == 0, kc == NKC - 1, [wtok, "cact"], [pst(pb)])
        dvl = ("dv", l)
        tt("dve", dv[:, l, 0:48], ps[pb][:, 0:48], pp[:, l, BADA:BADA + 48], ALU.add, ["pp"], [pst(pb), dvl])
        ts("dve", dv[:, l, SC1P:SC1P + 8], dv[:, l, 8:16], 1.0, None, ALU.add, None, [dvl], [(dvl, 1)])
        ts("pool", dv[:, l, G1P:G1P + 8], dv[:, l, 16:24], 1.0, None, ALU.add, None, [dvl], [(dvl, 2)])
        ts("dve", dv[:, l, SC2P:SC2P + 8], dv[:, l, 32:40], 1.0, None, ALU.add, None, [dvl], [(dvl, 3)])
        ts("pool", dv[:, l, G2P:G2P + 8], dv[:, l, 40:48], 1.0, None, ALU.add, None, [dvl], [(dvl, 4)])
        tt("dve", dv[:, l, BG1:BG1 + 8], dv[:, l, G1P:G1P + 8], pp[:, l, BOUT:BOUT + 8], ALU.mult,
           [(dvl, 2), "pp"], [(dvl, 5)])
        tt("dve", dv[:, l, BG2:BG2 + 8], dv[:, l, G2P:G2P + 8], pp[:, l, BDN:BDN + 8], ALU.mult,
           [(dvl, 4), "pp"], [(dvl, 6)])
        lam_init = 0.8 - 0.6 * math.exp(-0.3 * l)
        ts("pool", dv[:, l, SUBG2:SUBG2 + 1], pp[:, l, SUBG:SUBG + 1], 1.0 - lam_init, None, ALU.mult, None,
           ["pp"], [(dvl, 7)])
        tt("dve", dv[:, l, CF2:CF2 + 44], pp[:, l, CFW + 2:CFW + 132:3], pp[:, l, BUP:BUP + 44], ALU.mult,
           ["pp"], [(dvl, 8)])
        tt("dve", dv[:, l, CF2:CF2 + 44], dv[:, l, CF2:CF2 + 44], pp[:, l, CFB:CFB + 44], ALU.add,
           ["pp", (dvl, 8)], [(dvl, 8)])

    for l in range(n_layers):
        lam_init = 0.8 - 0.6 * math.exp(-0.3 * l)
        x_in = xT if l == 0 else xb
        last = (l == n_layers - 1)
        x_mid = yT if stop_phase == (l, 0) else xa
        x_out = yT if ((last and stop_phase is None) or stop_phase == (l, 1)) else xb
        col = lambda c0, n=1: pp[:, l, c0:c0 + n]
        dcol = lambda c0, n=1: dv[:, l, c0:c0 + n]
        P.fence()
        A.reset()
        kT = A.take([128, 4, S], BF16)
        Vc = A.take([128, S // 128, 512], BF16)
        win = A.take([128, NKC, INW], BF16)
        wout = A.take([128, NKC, D], BF16)
        rows = A.take([128, 1280], F32)
        wsp = A.take([128, 4, 128], BF16)
        bsp = A.take([128, 2, 128], F32)
        xt2 = [A.take([128, NKC, T], F32) for _ in range(2)]
        hb = A.take([128, NKC, T], BF16)
        cs_c = A.take([128, T], F32)
        cs_s = A.take([128, T], F32)
        yglu = A.take([128, 2, 30 + T], F32)
        cacc = A.take([128, 2, 2, T], F32)
        ftmp = [A.take([128, T], F32) for _ in range(6)]
        ftbf = [A.take([128, T], BF16) for _ in range(2)]
        off_tmp = A.off
        tmp = [A.take([128, T], F32) for _ in range(6)]
        off_end = A.off
        A.reset(off_tmp)
        OL = [A.take([128, 2 * T], F32) for _ in range(2)]
        A.reset(off_tmp)
        wspf = A.take([128, 4, 128], F32)
        lam_t = A.take([128, 256], F32)
        lam_w = A.take([128, 128], F32)
        assert A.off <= off_tmp + 4 * T * 4
        A.reset(off_end)
        yglu_bf = A.take([128, 2, 30 + T], BF16)
        tbf = [A.take([128, T], BF16) for _ in range(4)]
        mhalf = A.take([128, T], F32)
        qT2 = [A.take([128, 4, T], BF16) for _ in range(2)]
        uT = A.take([128, 2, T], F32)
        vcw = [A.take([128, 256], F32) for _ in range(2)]
        vn = A.take([128, NSUB, 256], BF16)
        st4 = A.take([128, 8], F32)
        PT = [A.take([128, 2, T], BF16) for _ in range(4)]
        ycat2 = [A.take([128, NKC, T], BF16) for _ in range(2)]

        for j in range(5):
            dma("pool", win[:, :, j * 512:(j + 1) * 512],
                w_in[l].rearrange("(kc p) n -> p kc n", p=128)[:, :, j * 512:(j + 1) * 512], [], [("win", j)])
        for kc in range(NKC):
            dma("pool", wout[:, kc, :], w_out[l, kc * 128:(kc + 1) * 128, :], [], [("wout", kc)])
        dma("sp", rows, rows_d[l], [], ["rows"])
        dma("sp", bsp, bsp_d[l], [], ["bsp"])
        dma("sp", wspf, wsp_d[l], [], ["tmp0", "tmp1"])
        for hh in range(4):
            tt("dve", wsp[:, hh, :], wspf[:, hh, :], tri_f, ALU.mult, ["tmp0", "tmp1", "cm"], [("wsp", hh)])
        dma("sp", lam_t, lamp_d[l], [], ["tmp2"])
        tt("dve", lam_w[:, 0:64], lam_t[:, 0:64], lam_t[:, 64:128], ALU.mult, ["tmp2"], ["tmp3"])
        tt("dve", lam_w[:, 64:128], lam_t[:, 128:192], lam_t[:, 192:256], ALU.mult, ["tmp2", "tmp3"], ["tmp3"])
        P.op("dve", lambda e, o=st4[:, 0:1], i=lam_w[:, 0:64]: e.tensor_reduce(
            out=o, in_=i, axis=mybir.AxisListType.X, op=ALU.add), ["tmp3"], ["st4a"])
        P.op("dve", lambda e, o=st4[:, 1:2], i=lam_w[:, 64:128]: e.tensor_reduce(
            out=o, in_=i, axis=mybir.AxisListType.X, op=ALU.add), ["tmp3"], ["st4b"])
        act(st4[:, 2:3], st4[:, 0:1], AF.Exp, ["st4a"], ["st4c"])
        act(st4[:, 3:4], st4[:, 1:2], AF.Exp, ["st4b"], ["st4d"])
        stt("dve", dcol(NEGLAM), st4[:, 3:4], -lam_init, st4[:, 2:3], ALU.add, ALU.subtract,
            ["st4c", "st4d"], [("dv", l, "neglam")])
        P.op("pool", lambda e, a=yglu[:, :, 0:30]: e.memset(a, 0.0), [], [("yglu", 0), ("yglu", 1)])
        P.op("pool", lambda e, a=yglu_bf[:, :, 0:30]: e.memset(a, 0.0), [], [("yglub", 0), ("yglub", 1)])
        P.op("pool", lambda e, a=mhalf: e.memset(a, -0.5), [], ["mhalf"])
        dg = Vc[:, 16:32, :].rearrange("p a b -> p (a b)")[:, 0:62 * 128].rearrange("p (k c) -> p k c", k=62)
        ident = cm[:, C_ID:C_ID + 128]
        for idx in range(62):
            ts("dve", dg[:, idx, :], ident, col(CAW + idx), None, ALU.mult, None, ["cm", "pp"], ["dg"])

        for i in range(n_tiles):
            t0 = i * T
            tsl = slice(t0, t0 + T)
            par = i % 2
            xt = xt2[par]
            qT = qT2[par]
            ycat = ycat2[par]
            XT = lambda c, par=par: ("xt", par, c)
            QT = lambda hd, par=par: ("qT", par, hd)
            YC = lambda c, par=par: ("ycat", par, c)
            dma("sp", xt, x_in.rearrange("(c p) t -> p c t", p=128)[:, :, tsl], [("xd", l, 0, i)],
                [XT(c) for c in range(NKC)])
            dma("sp", cs_c, cst[0, :, tsl], [("cst", t0 // CB)], ["cs_c"])
            dma("sp", cs_s, cst[1, :, tsl], [("cst", t0 // CB)], ["cs_s"])
            for c in range(NKC):
                ts(ve(), hb[:, c, :], xt[:, c, :], dcol(SC1P + c), dcol(c), ALU.mult, ALU.add,
                   [XT(c), ("dv", l), (("dv", l), 1)], [("hb", c)])
            hr = [("hb", c) for c in range(NKC)]

            def proj_fm(fc, pb):
                for kc in range(NKC):
                    mm(ps[pb][:, 0:T], win[:, kc, fc * 128:(fc + 1) * 128], hb[:, kc, :], kc == 0, kc == NKC - 1,
                       [("win", fc // 4), ("hb", kc)], [pst(pb)])

            for ch in range(2):
                pa = bank()
                proj_fm(ch, pa)
                act(ftmp[0], ps[pa][:, 0:T], AF.Identity, ["pp"], [pst(pa), "ftmp0"], bias=col(B_IN + ch))
                pg = bank()
                proj_fm(2 + ch, pg)
                act(ftmp[1], ps[pg][:, 0:T], AF.Sigmoid, ["pp"], [pst(pg), "ftmp1"], bias=col(B_IN + 2 + ch))
                tt("dve", yglu[:, ch, 30:30 + T], ftmp[0], ftmp[1], ALU.mult, ["ftmp0", "ftmp1"], [("yglu", ch)])
                if i < NPE_CONV:
                    act(yglu_bf[:, ch, 30:30 + T], yglu[:, ch, 30:30 + T], AF.Identity, [("yglu", ch)], [("yglub", ch)])
            if i < NPE_CONV:
                for ch in range(2):
                    pc = bank()
                    for k in range(31):
                        mm(ps[pc][:, 0:T], dg[:, ch * 31 + k, :], yglu_bf[:, ch, k:k + T], k == 0, k == 30,
                           ["dg", ("yglub", ch)], [pst(pc)])
                    act(cacc[:, ch, 0, :], ps[pc][:, 0:T], AF.Identity, ["pp"], [pst(pc), ("cacc", ch, 0)],
                        bias=col(CAB + ch))
                    cp("pool", yglu[:, ch, 0:30], yglu[:, ch, T:T + 30], [], [("yglu", ch)])
                    if i + 1 < NPE_CONV:
                        cp("pool", yglu_bf[:, ch, 0:30], yglu_bf[:, ch, T:T + 30], [], [("yglub", ch)])
            for k in (range(31) if i >= NPE_CONV else ()):
                for ch in range(2):
                    cpar = k % 2
                    eng = "dve"
                    wk = col(CAW + ch * 31 + k)
                    src = yglu[:, ch, k:k + T]
                    dst = cacc[:, ch, cpar, :]
                    tok = ("cacc", ch, cpar)
                    if k < 2:
                        if k == 0:
                            ts(eng, dst, src, wk, col(CAB + ch), ALU.mult, ALU.add, [("yglu", ch), "pp"], [tok])
                        else:
                            ts(eng, dst, src, wk, None, ALU.mult, None, [("yglu", ch), "pp"], [tok])
                    else:
                        stt(eng, dst, src, wk, dst, ALU.mult, ALU.add, [("yglu", ch), "pp"], [tok])
            for ch in (range(2) if i >= NPE_CONV else ()):
                eng = "dve" if ch == 0 else "pool"
                tt(eng, cacc[:, ch, 0, :], cacc[:, ch, 0, :], cacc[:, ch, 1, :], ALU.add,
                   [("cacc", ch, 1)], [("cacc", ch, 0)])
                cp(eng, yglu[:, ch, 0:30], yglu[:, ch, T:T + 30], [], [("yglu", ch)])
            for ch in range(2):
                y = cacc[:, ch, 0, :]
                ytok = ("cacc", ch, 0)
                cp("pool", ftbf[0], y, [ytok], ["ftbf0"])
                act(ftbf[1], y, AF.Square, [ytok], ["ftbf1"])
                p1 = bank()
                p2 = bank()
                mm(ps[p1][:, 0:T], blk_bf[:], ftbf[0], True, True, ["blk_bf", "ftbf0"], [pst(p1)])
                mm(ps[p2][:, 0:T], blk_bf[:], ftbf[1], True, True, ["blk_bf", "ftbf1"], [pst(p2)])
                ts("dve", ftmp[0], ps[p1][:, 0:T], 1.0 / 64, None, ALU.mult, None, [], [pst(p1), "ftmp0"])
                tt("pool", ftmp[1], ftmp[0], ftmp[0], ALU.mult, ["ftmp0"], ["ftmp1"])
                stt("dve", ftmp[2], ps[p2][:, 0:T], 1.0 / 64, ftmp[1], ALU.mult, ALU.subtract, ["ftmp1"],
                    [pst(p2), "ftmp2"])
                act(ftmp[2], ftmp[2], AF.Ln, ["ftmp2"], ["ftmp2"], bias=EPS)
                act(ftmp[2], ftmp[2], AF.Exp, ["ftmp2"], ["ftmp2"], scale=-0.5)
                tt("pool", ftmp[3], y, ftmp[0], ALU.subtract, [ytok, "ftmp0"], ["ftmp3"])
                tt("dve", ftmp[3], ftmp[3], ftmp[2], ALU.mult, ["ftmp3", "ftmp2"], ["ftmp3"])
                act(ycat[:, ch, :], ftmp[3], AF.Silu, ["ftmp3", "pp"], [YC(ch)], bias=col(GNB + ch),
                    scale=col(GNG + ch))
            for fc in range(4, 12):
                pb = bank()
                proj_fm(fc, pb)
                act(ftmp[4], ps[pb][:, 0:T], AF.Identity, ["pp"], [pst(pb), "ftmp4"], bias=col(B_IN + fc))
                pr = bank()
                mm(ps[pr][:, 0:T], perm, ftmp[4], True, True, ["cm", "ftmp4"], [pst(pr)])
                tt("pool", ftmp[5], ftmp[4], cs_c, ALU.mult, ["ftmp4", "cs_c"], ["ftmp5"])
                tt("dve", ftmp[3], ps[pr][:, 0:T], cs_s, ALU.mult, ["cs_s"], [pst(pr), "ftmp3"])
                if fc < 8:
                    hd = fc - 4
                    tt("pool", qT[:, hd, :], ftmp[5], ftmp[3], ALU.add, ["ftmp5", "ftmp3"], [QT(hd)])
                else:
                    hd = fc - 8
                    tt("pool", kT[:, hd, tsl], ftmp[5], ftmp[3], ALU.add, ["ftmp5", "ftmp3"], [("kT", hd, i)])
            for s in range(NSUB):
                blkno = t0 // 128 + s
                pb = bank()
                for kc in range(NKC):
                    mm(ps[pb][:, 0:512], hb[:, kc, s * 128:(s + 1) * 128], win[:, kc, 1536:2048], kc == 0,
                       kc == NKC - 1, [("win", 3), ("hb", kc)], [pst(pb)])
                tt("dve", Vc[:, blkno, :], ps[pb][:, 0:512], rows[:, 0:512], ALU.add, ["rows"],
                   [pst(pb), ("V", blkno)] + (["dg"] if blkno >= 16 else []))
                pb = bank()
                for kc in range(NKC):
                    mm(ps[pb][:, 0:256], hb[:, kc, s * 128:(s + 1) * 128], win[:, kc, 2304:2560], kc == 0,
                       kc == NKC - 1, [("win", 4), ("hb", kc)], [pst(pb)])
                w0, w1 = vcw
                tt("dve", w0, ps[pb][:, 0:256], rows[:, 512:768], ALU.add, ["rows"], [pst(pb), "vcw0"])
                act(w1, w0, AF.Gelu, ["vcw0"], ["vcw1", "st_a"], accum=st4[:, 4:5])
                ts("dve", st4[:, 5:6], st4[:, 4:5], -1.0 / 256, None, ALU.mult, None, ["st_a"], ["st_b"])
                act(w0, w1, AF.Square, ["vcw1", "st_b"], ["vcw0", "st_c"], bias=st4[:, 5:6], accum=st4[:, 6:7])
                ts("dve", st4[:, 7:8], st4[:, 6:7], 1.0 / 256, EPS, ALU.mult, ALU.add, ["st_c"], ["st_d"])
                tt("pool", st4[:, 7:8], st4[:, 7:8], mhalf[:, 0:1], ALU.pow, ["st_d", "mhalf"], ["st_d"])
                ts("dve", w1, w1, st4[:, 5:6], st4[:, 7:8], ALU.add, ALU.mult, ["vcw1", "st_b", "st_d"], ["vcw1"])
                tt("pool", w1, w1, rows[:, 768:1024], ALU.mult, ["vcw1", "rows"], ["vcw1"])
                tt("dve", vn[:, s, :], w1, rows[:, 1024:1280], ALU.add, ["vcw1", "rows"], [("vn", s)])
            for j in range(2):
                pb = bank()
                proj_fm(16 + j, pb)
                act(uT[:, j, :], ps[pb][:, 0:T], AF.Gelu, ["pp"], [pst(pb), ("uT", j)], bias=col(B_IN + 16 + j))
            for j in range(2):
                for s in range(NSUB):
                    pb = bank()
                    for hh in range(2):
                        hdx = 2 * j + hh
                        mm(ps[pb][hh * 64:(hh + 1) * 64, 0:128], vn[:, s, hdx * 64:(hdx + 1) * 64], wsp[:, hdx, :],
                           True, True, [("vn", s), ("wsp", hdx)], [pst(pb)])
                    tt("dve", ftmp[0][:, 0:128], ps[pb][:, 0:128], bsp[:, j, :], ALU.add, ["bsp"], [pst(pb), "ftmp0"])
                    tt("pool", ycat[:, 6 + j, s * 128:(s + 1) * 128], ftmp[0][:, 0:128], uT[:, j, s * 128:(s + 1) * 128],
                       ALU.mult, ["ftmp0", ("uT", j)], [YC(6 + j)])
            nkb = (t0 + T) // 128
            npair = nkb // 2
            npt = 0
            for hd in range(4):
                def qk(kp, hd=hd):
                    pAB = (sbank(), sbank())
                    for kk in range(2):
                        kb = 2 * kp + kk
                        for m in range(2):
                            mm(ps[pAB[m]][:, kk * T:(kk + 1) * T], kT[m * 64:(m + 1) * 64, hd, kb * 128:(kb + 1) * 128],
                               qT[m * 64:(m + 1) * 64, hd, :], True, True, [("kT", hd, (kb * 128) // T), QT(hd)],
                               [pst(pAB[m])])
                    return pAB
                for kp in range(npair):
                    pAB = qk(kp)
                    for m in range(2):
                        pt = PT[npt % len(PT)]
                        ptok = ("PT", npt % len(PT))
                        npt += 1
                        act(pt[:, :, :], ps[pAB[m]][:, 0:2 * T].rearrange("p (k t) -> p k t", k=2), AF.Exp,
                            [], [pst(pAB[m]), ptok], scale=0.125)
                        for kk in range(2):
                            kb = 2 * kp + kk
                            if kb * 128 >= t0:
                                c0 = kb * 128 - t0
                                tt("pool", pt[:, kk, c0:c0 + 128], pt[:, kk, c0:c0 + 128], tri2_bf[:, 0, :], ALU.mult,
                                   ["tri2_bf"], [ptok])
                        for kk in range(2):
                            kb = 2 * kp + kk
                            c0 = max(0, kb * 128 - t0)
                            mm(ps[m][:, c0:T], Vc[:, kb, hd * 128:(hd + 1) * 128], pt[:, kk, c0:T], kb == 0, False,
                               [("V", kb), ptok], [pst(m)])
                            mm(ps[m][:, T + c0:2 * T], ones_bf[:], pt[:, kk, c0:T], False, kb == nkb - 1,
                               ["ones_bf", ptok], [pst(m)])
                for m in range(2):
                    act(OL[m], ps[m][:, 0:2 * T], AF.Identity, [], [pst(m), "tmp%d" % (2 * m), "tmp%d" % (2 * m + 1)])
                for m in range(2):
                    tk2 = ["tmp%d" % (2 * m), "tmp%d" % (2 * m + 1)]
                    P.op("dve", lambda e, o=OL[m][:, T:2 * T]: e.reciprocal(out=o, in_=o), [], tk2, cost=0.6)
                    tt("dve", OL[m][:, 0:T], OL[m][:, 0:T], OL[m][:, T:2 * T], ALU.mult, [], tk2)
                stt("dve", OL[0][:, 0:T], OL[1][:, 0:T], dcol(NEGLAM), OL[0][:, 0:T], ALU.mult, ALU.add,
                    ["tmp2", "tmp3", ("dv", l, "neglam")], ["tmp0", "tmp1"])
                act(tbf[2], OL[0][:, 0:T], AF.Square, ["tmp0", "tmp1"], ["tbf2"])
                pb = sbank()
                mm(ps[pb][:, 0:T], ones_bf[:], tbf[2], True, True, ["ones_bf", "tbf2"], [pst(pb)])
                act(tmp[4], ps[pb][:, 0:T], AF.Ln, [], [pst(pb), "tmp4"], bias=EPS, scale=1.0 / 128)
                act(tmp[4], tmp[4], AF.Exp, ["tmp4"], ["tmp4"], scale=-0.5)
                stt("dve", ycat[:, 2 + hd, :], OL[0][:, 0:T], dcol(SUBG2), tmp[4], ALU.mult, ALU.mult,
                    ["tmp0", "tmp1", "tmp4", (("dv", l), 7)], [YC(2 + hd)])
            for oc in range(NKC):
                pb = bbank()
                for kc in range(NKC):
                    mm(ps[pb][:, 0:T], wout[:, kc, oc * 128:(oc + 1) * 128], ycat[:, kc, :], kc == 0, kc == NKC - 1,
                       [("wout", kc), YC(kc)], [pst(pb)])
                tk = "tmp%d" % (oc % 2)
                act(tmp[oc % 2], ps[pb][:, 0:T], AF.Identity, [(("dv", l), 2), (("dv", l), 5)], [pst(pb), tk],
                    bias=dcol(BG1 + oc), scale=dcol(G1P + oc))
                stt("dve", xt[:, oc, :], xt[:, oc, :], ALPHA, tmp[oc % 2], ALU.mult, ALU.add, [tk], [XT(oc)])
            layer_norm_tile(locals(), l, 0, XT)
            dma("sp", x_mid.rearrange("(c p) t -> p c t", p=128)[:, :, tsl], xt, [XT(c) for c in range(NKC)],
                [("xd", l, 1, i)])
        if stop_phase == (l, 0):
            break
        P.fence()
        A.reset()
        wup = A.take([128, NKC, 2 * FF], BF16)
        wdn = A.take([128, NHC, D], BF16)
        xt2 = [A.take([128, NKC, T], F32) for _ in range(2)]
        h2 = [A.take([128, NKC, T + 2], BF16) for _ in range(2)]
        ug = [A.take([128, T + 2], F32) for _ in range(2)]
        uv = [A.take([128, T + 2], F32) for _ in range(2)]
        cg = [A.take([128, T], F32) for _ in range(2)]
        cv = [A.take([128, T], F32) for _ in range(2)]
        sg = [A.take([128, T], F32) for _ in range(2)]
        hmid = A.take([128, NHC, T], BF16)
        tmp = [A.take([128, T], F32) for _ in range(6)]
        tbf = [A.take([128, T], BF16) for _ in range(4)]
        mhalf = A.take([128, T], F32)
        P.op("pool", lambda e, a=mhalf: e.memset(a, -0.5), [], ["mhalf"])
        for j in (0, 5, 6, 1, 7, 2, 8, 3, 9, 4, 10):
            dma("pool", wup[:, :, j * 512:(j + 1) * 512],
                w_up[l].rearrange("(kc p) n -> p kc n", p=128)[:, :, j * 512:(j + 1) * 512], [], [("wup", j)])
        for j in range(11):
            dma("pool", wdn[:, 2 * j:2 * j + 2, :],
                w_dn[l, j * 256:(j + 1) * 256, :].rearrange("(kc p) n -> p kc n", p=128), [], [("wdn", j)])
        P.op("pool", lambda e, a=h2[0][:, :, 0:2]: e.memset(a, 0.0), [], [("h2h", 0)])
        for i in range(n_tiles):
            t0 = i * T
            tsl = slice(t0, t0 + T)
            par = i % 2
            xt = xt2[par]
            XT = lambda c, par=par: ("xt", par, c)
            hcur = h2[i % 2]
            hnext = h2[(i + 1) % 2]
            dma("sp", xt, xa.rearrange("(c p) t -> p c t", p=128)[:, :, tsl], [("xd", l, 1, i)],
                [XT(c) for c in range(NKC)])
            for c in range(NKC):
                ts(ve(), hcur[:, c, 2:2 + T], xt[:, c, :], dcol(SC2P + c), dcol(24 + c), ALU.mult, ALU.add,
                   [XT(c), ("dv", l), (("dv", l), 3)], [("h2", i % 2, c)])
            cp("pool", hnext[:, :, 0:2], hcur[:, :, T:T + 2], [("h2", i % 2, c) for c in range(NKC)],
               [("h2h", (i + 1) % 2)])
            for c in range(NHC):
                b2 = c % 2
                pgk = fbank()
                for kc in range(NKC):
                    mm(ps[pgk][:, 0:T + 2], wup[:, kc, c * 128:(c + 1) * 128], hcur[:, kc, :], kc == 0, kc == NKC - 1,
                       [("wup", c // 4), ("h2", i % 2, kc), ("h2h", i % 2)], [pst(pgk)])
                act(ug[b2], ps[pgk][:, 0:T + 2], AF.Identity, ["pp"], [pst(pgk), ("ug", b2)], bias=col(BUP + c))
                pvk = fbank()
                cv_ = NHC + c
                for kc in range(NKC):
                    mm(ps[pvk][:, 0:T + 2], wup[:, kc, cv_ * 128:(cv_ + 1) * 128], hcur[:, kc, :], kc == 0,
                       kc == NKC - 1, [("wup", cv_ // 4), ("h2", i % 2, kc), ("h2h", i % 2)], [pst(pvk)])
                act(uv[b2], ps[pvk][:, 0:T + 2], AF.Identity, ["pp"], [pst(pvk), ("uv", b2)], bias=col(BUP + cv_))
                if i == 0:
                    P.op("pool", lambda e, a=ug[b2][:, 0:2]: e.memset(a, 0.0), [], [("ug", b2)])
                    P.op("pool", lambda e, a=uv[b2][:, 0:2]: e.memset(a, 0.0), [], [("uv", b2)])
                act(cg[b2], ps[pgk][:, 2:2 + T], AF.Identity, ["pp", (("dv", l), 8)], [pst(pgk), ("cg", b2)],
                    bias=dcol(CF2 + c), scale=col(CFW + c * 3 + 2))
                ts("pool", cv[b2], uv[b2][:, 2:2 + T], col(CFW + cv_ * 3 + 2), col(CFB + cv_), ALU.mult, ALU.add,
                   [("uv", b2), "pp"], [("cv", b2)])
                for k in (1, 0):
                    for (u_, c_, tok_u, tok_c, cc) in ((ug[b2], cg[b2], ("ug", b2), ("cg", b2), c),
                                                       (uv[b2], cv[b2], ("uv", b2), ("cv", b2), cv_)):
                        stt("dve", c_, u_[:, k:k + T], col(CFW + cc * 3 + k), c_, ALU.mult, ALU.add, [tok_u, "pp"], [tok_c])
                act(sg[b2], cg[b2], AF.Silu, [("cg", b2)], [("sg", b2)])
                tt("pool", hmid[:, c, :], sg[b2], cv[b2], ALU.mult, [("sg", b2), ("cv", b2)], [("hmid", c)])
            for oc in range(NKC):
                pb = fbank()
                for kc in range(NHC):
                    mm(ps[pb][:, 0:T], wdn[:, kc, oc * 128:(oc + 1) * 128], hmid[:, kc, :], kc == 0, kc == NHC - 1,
                       [("wdn", kc // 2), ("hmid", kc)], [pst(pb)])
                tk = "tmp%d" % (oc % 2)
                act(tmp[oc % 2], ps[pb][:, 0:T], AF.Identity, [(("dv", l), 4), (("dv", l), 6)], [pst(pb), tk],
                    bias=dcol(BG2 + oc), scale=dcol(G2P + oc))
                stt(ve(), xt[:, oc, :], xt[:, oc, :], ALPHA, tmp[oc % 2], ALU.mult, ALU.add, [tk], [XT(oc)])
            layer_norm_tile(locals(), l, 1, XT)
            dma("sp", x_out.rearrange("(c p) t -> p c t", p=128)[:, :, tsl], xt, [XT(c) for c in range(NKC)],
                [("xd", l + 1, 0, i)])
        if stop_phase == (l, 1):
            break
    P.emit(nc, es)
    es.close()
    return nc


def layer_norm_tile(env, l, which, XT):
    xt, ps, tmp, tbf, pp = env["xt"], env["ps"], env["tmp"], env["tbf"], env["pp"]
    ones_bf = env["ones_bf"]
    mm, act, ts, tt, stt, cp = env["mm"], env["act"], env["ts"], env["tt"], env["stt"], env["cp"]
    pst = env["pst"]
    gcol = LNG1 if which == 0 else LNG2
    bcol = LNB1 if which == 0 else LNB2
    p1, p2 = 0, 1
    for c in range(NKC):
        b = c % 2
        cp("dve" if c % 2 else "pool", tbf[b], xt[:, c, :], [XT(c)], ["tbf%d" % b])
        act(tbf[2 + b], xt[:, c, :], AF.Square, [XT(c)], ["tbf%d" % (2 + b)])
        mm(ps[p1][:, 0:T], ones_bf[:], tbf[b], c == 0, c == NKC - 1, ["ones_bf", "tbf%d" % b], [pst(p1)])
        mm(ps[p2][:, 0:T], ones_bf[:], tbf[2 + b], c == 0, c == NKC - 1, ["ones_bf", "tbf%d" % (2 + b)], [pst(p2)])
    ts("dve", tmp[2], ps[p1][:, 0:T], 1.0 / D, None, ALU.mult, None, [], [pst(p1), "tmp2"])
    tt("pool", tmp[3], tmp[2], tmp[2], ALU.mult, ["tmp2"], ["tmp3"])
    stt("dve", tmp[4], ps[p2][:, 0:T], 1.0 / D, tmp[3], ALU.mult, ALU.subtract, ["tmp3"], [pst(p2), "tmp4"])
    act(tmp[4], tmp[4], AF.Ln, ["tmp4"], ["tmp4"], bias=EPS)
    act(tmp[4], tmp[4], AF.Exp, ["tmp4"], ["tmp4"], scale=-0.5)
    for c in range(NKC):
        e1 = "pool" if c % 2 else "dve"
        e2 = "dve" if c % 2 else "pool"
        tt(e1, xt[:, c, :], xt[:, c, :], tmp[2], ALU.subtract, ["tmp2"], [XT(c)])
        tt(e2, xt[:, c, :], xt[:, c, :], tmp[4], ALU.mult, ["tmp4"], [XT(c)])
        ts(e1, xt[:, c, :], xt[:, c, :], pp[:, l, gcol + c:gcol + c + 1], pp[:, l, bcol + c:bcol + c + 1],
           ALU.mult, ALU.add, ["pp"], [XT(c)])


def _consts():
    cm = np.zeros((128, NCM), np.float32)
    j = np.arange(128)
    src = 64 * (j // 64) + ((j % 64) + 32) % 64
    cm[src, C_PERM + j] = 1.0
    cm[:, C_BLK:C_BLK + 128] = (j[:, None] // 64 == j[None, :] // 64)
    cm[:, C_ONES:C_ONES + 128] = 1.0
    cm[:, C_TRI:C_TRI + 128] = (j[:, None] <= j[None, :])
    inv_freq = (1.0 / (np.float32(10000.0) ** (np.arange(0, 64, 2, dtype=np.float32) / np.float32(64)))).astype(np.float32)
    cm[:, C_INVF] = inv_freq[(j % 64) % 32]
    cm[:, C_SGN] = np.where((j % 64) < 32, -1.0, 1.0)
    cm[j, C_ID + j] = 1.0
    return cm


def _cols(v):
    v = np.asarray(v, np.float32)
    return v.reshape(-1, 128).T


def _pack(inp):
    pp = np.zeros((128, L, NP), np.float32)
    rows = np.zeros((L, 128, 1280), np.float32)
    for l in range(L):
        pp[:, l, B_IN:B_IN + 20] = _cols(inp["b_in"][l])
        caw = np.asarray(inp["conv_a_w"][l], np.float32)
        for ch in range(2):
            pp[:, l, CAW + ch * 31:CAW + (ch + 1) * 31] = caw[:, ch * 128:(ch + 1) * 128].T
        pp[:, l, CAB:CAB + 2] = _cols(inp["conv_a_b"][l])
        pp[:, l, GNG:GNG + 2] = _cols(inp["gn_a_g"][l])
        pp[:, l, GNB:GNB + 2] = _cols(inp["gn_a_b"][l])
        pp[:, l, SUBG] = np.asarray(inp["subln_g"][l], np.float32)
        pp[:, l, BOUT:BOUT + 8] = _cols(inp["b_out"][l])
        pp[:, l, LNG1:LNG1 + 8] = _cols(inp["ln_g"][l, 0])
        pp[:, l, LNB1:LNB1 + 8] = _cols(inp["ln_b"][l, 0])
        pp[:, l, BUP:BUP + 44] = _cols(inp["b_up"][l])
        cfw = np.asarray(inp["conv_f_w"][l], np.float32)
        pp[:, l, CFW:CFW + 132] = cfw.reshape(3, 44, 128).transpose(2, 1, 0).reshape(128, 132)
        pp[:, l, CFB:CFB + 44] = _cols(inp["conv_f_b"][l])
        pp[:, l, BDN:BDN + 8] = _cols(inp["b_down"][l])
        pp[:, l, LNG2:LNG2 + 8] = _cols(inp["ln_g"][l, 1])
        pp[:, l, LNB2:LNB2 + 8] = _cols(inp["ln_b"][l, 1])
        pp[:, l, BADA:BADA + 48] = _cols(inp["b_ada"][l])
        b_in = np.asarray(inp["b_in"][l], np.float32)
        rows[l, :, 0:512] = b_in[None, 1536:2048]
        rows[l, :, 512:768] = b_in[None, 2304:2560]
        rows[l, :, 768:1024] = np.asarray(inp["ln_c_g"][l], np.float32)[None]
        rows[l, :, 1024:1280] = np.asarray(inp["ln_c_b"][l], np.float32)[None]
    lamp = np.ascontiguousarray(np.broadcast_to(np.asarray(inp["lam_p"], np.float32).reshape(L, 1, 256), (L, 128, 256)))
    wsp = np.ascontiguousarray(np.asarray(inp["w_sp"], np.float32).transpose(0, 3, 1, 2))
    b_sp = np.asarray(inp["b_sp"], np.float32)
    bsp = np.ascontiguousarray(np.repeat(b_sp.reshape(L, 2, 2, 1, 128), 64, axis=3).reshape(L, 2, 128, 128)
                               .transpose(0, 2, 1, 3))
    return pp, rows, lamp, wsp, bsp


def _run(inputs, n_layers=L, n_tiles=S // T, stop_phase=None, cores=8, trace=False):
    inp = {k: np.asarray(v) for k, v in inputs.items()}
    pp, rows, lamp, wsp, bsp = _pack(inp)
    cm = _consts()
    shared = {
        "cm": cm, "pp": pp, "rows": rows, "lamp": lamp, "wspT": wsp, "bsp": bsp,
        "w_ada": np.ascontiguousarray(inp["w_ada"], np.float32), "w_in": np.ascontiguousarray(inp["w_in"], np.float32),
        "w_out": np.ascontiguousarray(inp["w_out"], np.float32), "w_up": np.ascontiguousarray(inp["w_up"], np.float32),
        "w_down": np.ascontiguousarray(inp["w_down"], np.float32),
    }
    in_maps = []
    for b in range(cores):
        m = dict(shared)
        m["xT"] = np.ascontiguousarray(inp["x"][b].T.astype(np.float32, copy=False))
        m["pos"] = np.ascontiguousarray(inp["positions"][b].reshape(1, S).astype(np.int32, copy=False))
        m["cvec"] = np.ascontiguousarray(_cols(inp["c"][b]))
        in_maps.append(m)
    nc = build(n_layers, n_tiles, stop_phase)
    res = run_bass_kernel_spmd(nc, in_maps, core_ids=list(range(cores)), trace=trace)
    outs = [np.asarray(r["yT"]) for r in res.results]
    return outs, res


def kernel(**inputs):
    outs, _ = _run(inputs)
    return np.stack([o.T for o in outs], axis=0).astype(np.float32)
```
